# Optimizing a Trainium2 kernel written in Bass

```python
import math
import jax, jax.numpy as jnp
from jax import lax
import numpy as np

D_MODEL = 1024
BATCH = 4
SEQ = 4096
DEPTH = 1

MIX_WIDTH = D_MODEL
D_CONV = MIX_WIDTH // 2
D_SSM = MIX_WIDTH - D_CONV
CONV_HEAD_DIM = 64
CONV_HEADS = D_CONV // CONV_HEAD_DIM
CONV_WIDTH = 3
SSM_GROUP_CH = 16
SSM_GROUPS = D_SSM // SSM_GROUP_CH
SSM_STATE = 64
D_FF = 256 * ((8 * D_MODEL // 3 + 255) // 256)
N_MOD = 9
EPS = 1e-6
DT_MIN = 1e-3
DT_MAX = 1e-1

kernel_name = "hymba_conv_s5_macaron_adaln_layer"


def rms_norm(x, gain):
    xf = x.astype(jnp.float32)
    y = xf * lax.rsqrt(jnp.mean(xf * xf, axis=-1, keepdims=True) + EPS)
    return (y * gain.astype(jnp.float32)).astype(x.dtype)


def modulate(h, shift, scale):
    return h * (1 + scale[:, None, :]) + shift[:, None, :]


def swiglu(h, w_gate, w_up, w_down):
    return (jax.nn.silu(h @ w_gate) * (h @ w_up)) @ w_down


def short_conv_mixer(b_gate, c_gate, v, conv_w):
    z = c_gate * v
    z = lax.conv_general_dilated(
        z, conv_w[:, None, :].astype(z.dtype),
        window_strides=(1,), padding=[(CONV_WIDTH - 1, 0)],
        dimension_numbers=('NWC', 'WIO', 'NWC'),
        feature_group_count=D_CONV)
    return b_gate * z


def _ssm_combine(left, right):
    a_l, b_l = left
    a_r, b_r = right
    return a_r * a_l, a_r * b_l + b_r


def s5_mixer(u, lambda_re, lambda_im, log_dt, b_re, b_im, c_re, c_im, d_skip, glu_w, glu_b):
    bsz, seq, _ = u.shape
    uf = u.astype(jnp.float32).reshape(bsz, seq, SSM_GROUPS, SSM_GROUP_CH)
    lam = lax.complex(lambda_re.astype(jnp.float32), lambda_im.astype(jnp.float32))
    dt = jnp.exp(log_dt.astype(jnp.float32))[:, None]
    lam_bar = jnp.exp(lam * dt)
    b = lax.complex(b_re.astype(jnp.float32), b_im.astype(jnp.float32))
    b_bar = ((lam_bar - 1) / lam)[..., None] * b
    bu = jnp.einsum('gpc,bsgc->bsgp', b_bar, uf.astype(jnp.complex64))
    a = jnp.broadcast_to(lam_bar, bu.shape)
    _, states = lax.associative_scan(_ssm_combine, (a, bu), axis=1)
    c = lax.complex(c_re.astype(jnp.float32), c_im.astype(jnp.float32))
    y = jnp.einsum('gcp,bsgp->bsgc', c, states).real \
        + d_skip.astype(jnp.float32).reshape(SSM_GROUPS, SSM_GROUP_CH) * uf
    y = jax.nn.gelu(y.reshape(bsz, seq, D_SSM))
    y = y * jax.nn.sigmoid(y @ glu_w.astype(jnp.float32) + glu_b.astype(jnp.float32))
    return y.astype(u.dtype)


def setup_inputs(seed: int = 0) -> dict:
    key = jax.random.key(seed)
    ks = jax.random.split(key, 32)
    f32 = jnp.float32

    def nrm(k, shape, std):
        return jax.random.normal(k, shape, f32) * std

    def gain(k, shape):
        return 1.0 + 0.05 * jax.random.normal(k, shape, f32)

    L = DEPTH
    n_idx = jnp.arange(SSM_STATE, dtype=f32)
    return {
        "x": nrm(ks[0], (BATCH, SEQ, D_MODEL), 1.0),
        "cond": nrm(ks[1], (BATCH, D_MODEL), 1.0),
        "w_mod": nrm(ks[2], (L, D_MODEL, N_MOD * D_MODEL), 0.5 * D_MODEL ** -0.5),
        "b_mod": nrm(ks[3], (L, N_MOD * D_MODEL), 0.02),
        "ffn1_norm": gain(ks[4], (L, D_MODEL)),
        "ffn1_w_gate": nrm(ks[5], (L, D_MODEL, D_FF), D_MODEL ** -0.5),
        "ffn1_w_up": nrm(ks[6], (L, D_MODEL, D_FF), D_MODEL ** -0.5),
        "ffn1_w_down": nrm(ks[7], (L, D_FF, D_MODEL), D_FF ** -0.5),
        "mix_norm": gain(ks[8], (L, D_MODEL)),
        "w_in": nrm(ks[9], (L, D_MODEL, 3 * D_CONV + D_SSM), D_MODEL ** -0.5),
        "conv_w": nrm(ks[10], (L, CONV_WIDTH, D_CONV), CONV_WIDTH ** -0.5),
        "lambda_re": -0.5 + 0.01 * jax.random.normal(ks[11], (L, SSM_GROUPS, SSM_STATE), f32),
        "lambda_im": math.pi * n_idx + 0.01 * jax.random.normal(ks[12], (L, SSM_GROUPS, SSM_STATE), f32),
        "log_dt": jax.random.uniform(ks[13], (L, SSM_GROUPS), f32, math.log(DT_MIN), math.log(DT_MAX)),
        "ssm_b_re": nrm(ks[14], (L, SSM_GROUPS, SSM_STATE, SSM_GROUP_CH), (2 * SSM_GROUP_CH) ** -0.5),
        "ssm_b_im": nrm(ks[15], (L, SSM_GROUPS, SSM_STATE, SSM_GROUP_CH), (2 * SSM_GROUP_CH) ** -0.5),
        "ssm_c_re": nrm(ks[16], (L, SSM_GROUPS, SSM_GROUP_CH, SSM_STATE), SSM_STATE ** -0.5),
        "ssm_c_im": nrm(ks[17], (L, SSM_GROUPS, SSM_GROUP_CH, SSM_STATE), SSM_STATE ** -0.5),
        "ssm_d": nrm(ks[18], (L, D_SSM), 1.0),
        "glu_w": nrm(ks[19], (L, D_SSM, D_SSM), D_SSM ** -0.5),
        "glu_b": nrm(ks[20], (L, D_SSM), 0.02),
        "out_norm_conv": gain(ks[21], (L, D_CONV)),
        "out_norm_ssm": gain(ks[22], (L, D_SSM)),
        "w_out": nrm(ks[23], (L, MIX_WIDTH, D_MODEL), MIX_WIDTH ** -0.5),
        "ffn2_norm": gain(ks[24], (L, D_MODEL)),
        "ffn2_w_gate": nrm(ks[25], (L, D_MODEL, D_FF), D_MODEL ** -0.5),
        "ffn2_w_up": nrm(ks[26], (L, D_MODEL, D_FF), D_MODEL ** -0.5),
        "ffn2_w_down": nrm(ks[27], (L, D_FF, D_MODEL), D_FF ** -0.5),
        "final_norm": gain(ks[28], (D_MODEL,)),
    }


def reference(x, cond, w_mod, b_mod, ffn1_norm, ffn1_w_gate, ffn1_w_up, ffn1_w_down,
              mix_norm, w_in, conv_w, lambda_re, lambda_im, log_dt,
              ssm_b_re, ssm_b_im, ssm_c_re, ssm_c_im, ssm_d, glu_w, glu_b,
              out_norm_conv, out_norm_ssm, w_out,
              ffn2_norm, ffn2_w_gate, ffn2_w_up, ffn2_w_down, final_norm):
    c_act = jax.nn.silu(cond)
    for l in range(DEPTH):
        mod = c_act @ w_mod[l] + b_mod[l]
        (sh1, sc1, g1, sh2, sc2, g2, sh3, sc3, g3) = jnp.split(mod, N_MOD, axis=-1)

        h = modulate(rms_norm(x, ffn1_norm[l]), sh1, sc1)
        x = x + 0.5 * g1[:, None, :] * swiglu(h, ffn1_w_gate[l], ffn1_w_up[l], ffn1_w_down[l])

        h = modulate(rms_norm(x, mix_norm[l]), sh2, sc2)
        proj = h @ w_in[l]
        b_gate, c_gate, v, u = jnp.split(proj, [D_CONV, 2 * D_CONV, 3 * D_CONV], axis=-1)
        y_conv = short_conv_mixer(b_gate, c_gate, v, conv_w[l])
        y_ssm = s5_mixer(u, lambda_re[l], lambda_im[l], log_dt[l], ssm_b_re[l], ssm_b_im[l],
                         ssm_c_re[l], ssm_c_im[l], ssm_d[l], glu_w[l], glu_b[l])
        y_mix = jnp.concatenate([rms_norm(y_conv, out_norm_conv[l]),
                                 rms_norm(y_ssm, out_norm_ssm[l])], axis=-1)
        x = x + g2[:, None, :] * (y_mix @ w_out[l])

        h = modulate(rms_norm(x, ffn2_norm[l]), sh3, sc3)
        x = x + 0.5 * g3[:, None, :] * swiglu(h, ffn2_w_gate[l], ffn2_w_up[l], ffn2_w_down[l])
    return rms_norm(x, final_norm)
```

```python
import math
from contextlib import ExitStack

import numpy as np
import concourse.bass as bass
import concourse.mybir as mybir
from concourse.bass_utils import run_bass_kernel_spmd

F32, BF16, I32 = mybir.dt.float32, mybir.dt.bfloat16, mybir.dt.int32
ALU = mybir.AluOpType
AF = mybir.ActivationFunctionType
ENGS = ["pe", "act", "dve", "pool", "sp"]
TWO_PI = 2.0 * math.pi
EPS = 1e-6
STRICT = True
NO_CC = False
MLIST = [1, -8, 8, 128] + [8 * m for m in range(16)] + [128 * k for k in range(16)]
NK = 32 + 2 * len(MLIST)


class Op:
    pass


class Prog:
    def __init__(self, nc, es):
        self.nc, self.es = nc, es
        self.ops = []
        self.eng_ops = {e: [] for e in ENGS}
        self.lastw, self.readers = {}, {}
        self.dma_cnt, self.sems = {}, {}
        self.last_dma = {}
        self.pending = {e: [] for e in ENGS}

    def semh(self, key):
        if key not in self.sems:
            self.sems[key] = self.es.enter_context(self.nc.semaphore("s%d" % len(self.sems)))
        return self.sems[key]

    def add(self, eng, fn, reads=(), writes=(), silent=False, dma=None):
        op = Op()
        op.eng, op.fn, op.silent = eng, fn, silent
        op.gidx, op.eidx = len(self.ops), len(self.eng_ops[eng])
        op.dma_sem, op.dma_val, op.sig = None, 0, 0
        reads = list(reads) + self.pending[eng]
        self.pending[eng] = []
        if dma is not None:
            self.last_dma[dma] = op
            op.dma_sem = dma
            self.dma_cnt[dma] = self.dma_cnt.get(dma, 0) + 1
            op.dma_val = 16 * self.dma_cnt[dma]
            self.semh(("d", dma))
        deps = []
        for k in reads:
            w = self.lastw.get(k)
            if w is not None:
                deps.append(w)
        inord = (lambda o: (not STRICT or eng == "pe") and dma is None and o.dma_sem is None and o.eng == eng
                 and eng in ("pe", "act", "dve"))
        for k in writes:
            w = self.lastw.get(k)
            if w is not None and not inord(w):
                deps.append(w)
            for r in self.readers.get(k, ()):
                if not inord(r):
                    deps.append(r)
        op.deps = deps
        for k in reads:
            self.readers.setdefault(k, []).append(op)
        for k in writes:
            self.lastw[k] = op
            self.readers[k] = []
        self.ops.append(op)
        self.eng_ops[eng].append(op)
        return op

    def barrier(self, tag):
        keys = []
        for e in ENGS:
            lastc = None
            for op in reversed(self.eng_ops[e]):
                if op.dma_sem is None:
                    lastc = op
                    break
            if lastc is not None:
                lastc.silent = False
                k = ("bar", tag, e)
                self.lastw[k] = lastc
                self.readers[k] = []
                keys.append(k)
        for sem, op in self.last_dma.items():
            k = ("bar", tag, "d", sem)
            self.lastw[k] = op
            self.readers[k] = []
            keys.append(k)
        for e in ENGS:
            self.pending[e] = self.pending[e] + keys

    def finalize(self, block):
        for op in self.ops:
            res = []
            for P in op.deps:
                if P.dma_sem is not None or not P.silent:
                    res.append(P)
                    continue
                Q = None
                for q in self.eng_ops[P.eng][P.eidx:]:
                    if q.gidx >= op.gidx:
                        break
                    if q.dma_sem is None and not q.silent:
                        Q = q
                        break
                if Q is None:
                    P.silent = False
                    Q = P
                res.append(Q)
            op.deps = res
        for e in ENGS:
            c = 0
            for op in self.eng_ops[e]:
                if op.dma_sem is None and not op.silent:
                    c += 1
                    op.sig = c
            self.semh(("e", e))

        def emit(e, eng):
            waited = {}
            for op in self.eng_ops[e]:
                need = {}
                for P in op.deps:
                    if P.dma_sem is not None:
                        s, v = ("d", P.dma_sem), P.dma_val
                    else:
                        s, v = ("e", P.eng), P.sig
                    if v > need.get(s, 0):
                        need[s] = v
                for s, v in need.items():
                    if waited.get(s, 0) < v:
                        eng.wait_ge(self.semh(s), v)
                        waited[s] = v
                ins = op.fn(eng)
                if ins is None:
                    continue
                if op.dma_sem is not None:
                    ins.then_inc(self.semh(("d", op.dma_sem)), 16)
                elif not op.silent:
                    ins.then_inc(self.semh(("e", e)), 1)

        @block.tensor
        def _(g):
            emit("pe", g)

        @block.scalar
        def _(g):
            emit("act", g)

        @block.vector
        def _(g):
            emit("dve", g)

        @block.gpsimd
        def _(g):
            emit("pool", g)

        @block.sync
        def _(g):
            emit("sp", g)


def V(t, off, dims, npart=128):
    base = t[:]
    return bass.AP(base.tensor, base.offset + off, [[base.ap[0][0], npart]] + [list(d) for d in dims])


def build(ncores=8):
    nc = bass.Bass("TRN2", target_bir_lowering=False)
    es = ExitStack()
    P = Prog(nc, es)
    global _LASTP
    _LASTP = P

    def din(name, shape, dt=F32):
        return nc.dram_tensor(name, list(shape), dt, kind="ExternalInput").ap()

    x = din("x", [2048, 1024]); xprev = din("xprev", [2048, 1024]); cact_in = din("cact_in", [128, 8])
    w_mod = din("w_mod", [128, 73728]); b_mod = din("b_mod", [1, 9216])
    nrm = [din("n%d" % i, [128, 8]) for i in (1, 2, 3)]; fin = din("fin", [1, 1024])
    wg = [din("wg%d" % i, [128, 22528]) for i in (1, 2)]
    wu = [din("wu%d" % i, [128, 22528]) for i in (1, 2)]
    wdn = [din("wd%d" % i, [128, 22528]) for i in (1, 2)]
    w_in = din("w_in", [128, 16384]); convw = din("convw", [128, 12])
    lamre = din("lamre", [128, 32]); lamim = din("lamim", [128, 32]); logdt = din("logdt", [128, 32])
    bre = din("bre", [128, 512]); bim = din("bim", [128, 512])
    cre = din("cre", [128, 512]); cim = din("cim", [128, 512]); dcol = din("dcol", [128, 32])
    gluw = din("gluw", [128, 2048]); glub = din("glub", [1, 512])
    onc = din("onc", [128, 4]); ons = din("ons", [128, 4]); wout = din("wout", [128, 8192])
    identf_d = din("identf", [128, 128]); maskc_d = din("maskc", [128, 128]); sel_d = din("sel", [128, 8])
    mtab_d = din("mtab", [128, NK]); ptab_d = din("ptab", [128, NK])
    y = nc.dram_tensor("y", [2048, 1024], F32, kind="ExternalOutput").ap()
    ssmw = nc.dram_tensor("ssmw", [4, 128, 4096], BF16, kind="Internal").ap()
    xs = nc.dram_tensor("xs", [4, 128, 8192], F32, kind="Internal").ap()
    u8s = nc.dram_tensor("u8s", [4, 128, 4096], BF16, kind="Internal").ap()
    hlas = nc.dram_tensor("hlas", [4, 128, 4096], BF16, kind="Internal").ap()
    gts = nc.dram_tensor("gts", [3, 128, 1024], F32, kind="Internal").ap()
    cc_in = nc.dram_tensor("cc_in", [128, 72], F32, kind="Internal").ap()
    cc_out = nc.dram_tensor("cc_out", [ncores * 128, 72], F32, kind="Internal").ap()

    _nm = [0]

    def sb(name, shape, dt=F32, st=None):
        _nm[0] += 1
        return (st or es).enter_context(nc.sbuf_tensor("s%d_%s" % (_nm[0], name), list(shape), dt))

    def ps(name, shape, dt=F32):
        return es.enter_context(nc.psum_tensor("p_" + name, list(shape), dt))

    ring = [sb("ring%d" % i, [128, 8, 512], BF16) for i in range(3)]
    Gc = sb("Gc", [128, 1024])
    identf = sb("identf", [128, 128]); identb = sb("identb", [128, 128], BF16)
    onesf = sb("onesf", [128, 128]); onesb = sb("onesb", [1, 128], BF16)
    Gm = sb("Gm", [128, 24]); modT = sb("modT", [128, 72])
    convw_s = sb("convw_s", [128, 12]); onc_s = sb("onc_s", [128, 4]); ons_s = sb("ons_s", [128, 4])
    sel_s = sb("sel_s", [128, 8]); glub_b = sb("glub_b", [1, 512], BF16)
    EP = sb("EP", [128, 32, 2 * len(MLIST)])
    HEA = sb("HEA", [128, 32, 32]); HEB = sb("HEB", [128, 32, 32])
    HIA = sb("HIA", [128, 32, 33]); HIB = sb("HIB", [128, 32, 33])
    PAY = sb("PAY", [128, 72]); HIN = sb("HIN", [128, 72])
    zlast = sb("zlast", [128, 4, 8], BF16)
    sst = sb("sst", [128, 64])
    pb = [ps("pb%d" % i, [128, 512]) for i in range(7)]
    pt = ps("pt", [128, 1024], BF16)
    NL = len(MLIST)

    cnt = {"ring": 0, "xn": 0, "ss": 0, "sg": 0, "ta": 0, "gu": 0, "dn": 0}

    def rot(name, n):
        v = cnt[name] % n
        cnt[name] += 1
        return v

    def dve(fn, reads, writes):
        return P.add("dve", fn, reads, writes)

    def act(fn, reads, writes):
        return P.add("act", fn, reads, writes)

    def load(eng, out_ap, in_ap, key, sem):
        return P.add(eng, lambda g: g.dma_start(out=out_ap, in_=in_ap), writes=[key], dma=sem)

    xr = x.rearrange("(h s j) d -> s h j d", h=2, s=128, j=8)
    xpr = xprev.rearrange("(h s j) d -> s h j d", h=2, s=128, j=8)

    st = ExitStack()
    cact = sb("cact", [128, 8], F32, st); cactb = sb("cactb", [128, 8], BF16, st)
    modrow = [sb("modrow%d" % i, [1, 512], F32, st) for i in range(2)]
    bmod_s = [sb("bmods%d" % i, [1, 512], F32, st) for i in range(2)]
    Gs = sb("Gs", [128, 1024], F32, st)
    nrm_s = sb("nrm_s", [128, 24], F32, st)
    one11 = sb("one11", [1, 1], F32, st)
    lre = sb("lre", [128, 32], F32, st); lim = sb("lim", [128, 32], F32, st); ldt = sb("ldt", [128, 32], F32, st)
    Br = sb("Br", [128, 32, 16], F32, st); Bi = sb("Bi", [128, 32, 16], F32, st)
    Cr = sb("Cr", [128, 32, 16], F32, st); Ci = sb("Ci", [128, 32, 16], F32, st)
    dcol_s = sb("dcol_s", [128, 32], F32, st); maskc = sb("maskc", [128, 128], F32, st)
    mtab = sb("mtab", [128, NK], F32, st); ptab = sb("ptab", [128, NK], F32, st)
    glub_f = sb("glub_f", [1, 512], F32, st)
    TAB = sb("TAB", [128, 32, NK], F32, st); T1 = sb("T1", [128, 32, NK], F32, st)
    T2 = sb("T2", [128, 32, NK], F32, st); TI = sb("TI", [128, 32, NK], I32, st)
    adt = sb("adt", [128, 32], F32, st); bdt = sb("bdt", [128, 32], F32, st)
    sm = [sb("sm%d" % i, [128, 32], F32, st) for i in range(8)]
    WA = sb("WA", [128, 32, 8], F32, st); WB = sb("WB", [128, 32, 8], F32, st)
    WA2 = sb("WA2", [128, 32, 8], F32, st); WB2 = sb("WB2", [128, 32, 8], F32, st)
    W8 = [sb("W8%d" % i, [128, 32, 8], F32, st) for i in range(2)]
    LBSA = sb("LBSA", [128, 8, 128], F32, st); LBSB = sb("LBSB", [128, 8, 128], F32, st)
    QS = sb("QS", [128, 8, 128], F32, st); RS = sb("RS", [128, 8, 128], F32, st)
    OT = [sb("OT%d" % i, [128, 8, 128], F32, st) for i in range(2)]
    STG = [sb("STG%d" % i, [128, 8, 128], BF16, st) for i in range(4)]
    KTt = sb("KTt", [128, 128], F32, st)

    smalls = [(cact, cact_in, "cact"), (lre, lamre, "lre"), (lim, lamim, "lim"),
              (ldt, logdt, "ldt"), (dcol_s, dcol, "dcol"), (maskc, maskc_d, "maskc"), (mtab, mtab_d, "mtab"),
              (ptab, ptab_d, "ptab"), (glub_f, glub, "glubf"), (identf, identf_d, "identf"),
              (convw_s, convw, "convw"), (onc_s, onc, "onc"), (ons_s, ons, "ons"), (sel_s, sel_d, "sel")]
    last = None
    keys = []
    for t, src, k in smalls:
        last = P.add("sp", lambda g, t=t, src=src: g.dma_start(out=t[:], in_=src), writes=[k], dma="small")
        keys.append(k)
    for i in range(3):
        last = P.add("sp", lambda g, i=i: g.dma_start(out=nrm_s[:, i * 8:(i + 1) * 8], in_=nrm[i]),
                     writes=["nrm%d" % i], dma="small")
        keys.append("nrm%d" % i)
    for t, src, k in ((Br, bre, "Br"), (Bi, bim, "Bi"), (Cr, cre, "Cr"), (Ci, cim, "Ci")):
        last = P.add("sp", lambda g, t=t, src=src: g.dma_start(out=t[:].rearrange("p a b -> p (a b)"), in_=src),
                     writes=[k], dma="small")
        keys.append(k)
    for k in keys:
        P.lastw[k] = last

    dve(lambda g: g.memset(onesf[:], 1.0), [], ["onesf"])
    dve(lambda g: g.memset(onesb[:], 1.0), [], ["onesb"])
    dve(lambda g: g.memset(one11[:], 1.0), [], ["one11"])
    dve(lambda g: g.tensor_copy(out=identb[:], in_=identf[:]), ["identf"], ["identb"])
    dve(lambda g: g.tensor_copy(out=glub_b[:], in_=glub_f[:]), ["glubf"], ["glubb"])
    act(lambda g: g.activation(out=cact[:], in_=cact[:], func=AF.Silu), ["cact"], ["cact"])
    dve(lambda g: g.tensor_copy(out=cactb[:], in_=cact[:]), ["cact"], ["cactb"])

    def cast_load(dst_ap, src_ap, dst_key, a, b):
        P.add("pool", lambda g: g.dma_start(out=dst_ap, in_=src_ap), writes=[dst_key], dma="cl_" + str(dst_key))

    def ring_load(src_ap, key):
        s = rot("ring", 3)
        cast_load(ring[s][:], src_ap, ("ring", s), 8, 512)
        return s

    gate_blk = {4: (0, 0, 0.5), 5: (0, 1, 0.5), 10: (1, 0, 1.0), 11: (1, 1, 1.0), 16: (2, 0, 0.5), 17: (2, 1, 0.5)}
    for blk in range(18):
        s = ring_load(w_mod[:, blk * 4096:(blk + 1) * 4096].rearrange("p (k n) -> p k n", k=8), None)
        bank = pb[blk % 2]
        mi = blk % 2
        P.add("sp", lambda g, blk=blk, mi=mi: g.dma_start(out=bmod_s[mi][:], in_=b_mod[0:1, blk * 512:(blk + 1) * 512]),
              writes=[("bmod", mi)], dma="bmod%d" % mi)
        for kc in range(8):
            P.add("pe", lambda g, kc=kc, s=s, bank=bank: g.matmul(bank[0:1, :], lhsT=cactb[:, kc:kc + 1],
                                                                    rhs=ring[s][:, kc, :], start=(kc == 0), stop=(kc == 7)),
                  reads=[("ring", s), "cactb"], writes=[("pb", blk % 2)], silent=(kc < 7))
        dve(lambda g, mi=mi, bank=bank: g.tensor_tensor(out=modrow[mi][:], in0=bank[0:1, :], in1=bmod_s[mi][:], op=ALU.add),
            [("pb", blk % 2), ("bmod", mi)], [("modrow", mi)])
        for q4 in range(4):
            q = blk * 4 + q4
            P.add("pe", lambda g, q=q, q4=q4, mi=mi: g.matmul(pb[2][:, q:q + 1], lhsT=modrow[mi][0:1, q4 * 128:(q4 + 1) * 128],
                                                              rhs=one11[0:1, 0:1], start=True, stop=True),
                  reads=[("modrow", mi), "one11"], writes=[("pb2c", q)], silent=(q4 < 3))
        if blk in gate_blk:
            gi_, nh, sc = gate_blk[blk]
            P.add("pe", lambda g, mi=mi: g.matmul(pb[3][:, :], lhsT=onesf[0:1, :], rhs=modrow[mi][:], start=True, stop=True),
                  reads=[("modrow", mi), "onesf"], writes=[("pb", 3)])
            act(lambda g, nh=nh, sc=sc: g.mul(Gs[:, nh * 512:(nh + 1) * 512], pb[3][:, :], sc), [("pb", 3)], ["Gs"])
            if nh == 1:
                P.add("sp", lambda g, gi_=gi_: g.dma_start(out=gts[gi_], in_=Gs[:]), reads=["Gs"], writes=[("gts", gi_)],
                      dma="gts")
    dve(lambda g: g.tensor_copy(out=modT[:], in_=pb[2][:, 0:72]), [("pb2c", q) for q in range(72)], ["modT"])
    for n in range(3):
        dve(lambda g, n=n: g.scalar_tensor_tensor(out=Gm[:, n * 8:(n + 1) * 8], in0=modT[:, 24 * n + 8:24 * n + 16],
                                                  scalar=1.0, in1=nrm_s[:, n * 8:(n + 1) * 8], op0=ALU.add, op1=ALU.mult),
            ["modT", "nrm%d" % n], ["Gm"])

    act(lambda g: g.activation(out=ldt[:], in_=ldt[:], func=AF.Exp), ["ldt"], ["dt"])
    dve(lambda g: g.tensor_tensor(out=adt[:], in0=lre[:], in1=ldt[:], op=ALU.mult), ["lre", "dt"], ["adt"])
    dve(lambda g: g.tensor_tensor(out=bdt[:], in0=lim[:], in1=ldt[:], op=ALU.mult), ["lim", "dt"], ["bdt"])
    bG = lambda t: V(t, 0, [[1, 32], [0, NK]])
    bK = lambda t: V(t, 0, [[0, 32], [1, NK]])
    dve(lambda g: g.tensor_tensor(out=T1[:], in0=bG(bdt), in1=bK(mtab), op=ALU.mult), ["bdt", "mtab"], ["T1"])
    dve(lambda g: g.tensor_tensor(out=T1[:], in0=T1[:], in1=bK(ptab), op=ALU.add), ["T1", "ptab"], ["T1"])
    dve(lambda g: g.tensor_scalar(out=T2[:], in0=T1[:], scalar1=1.0 / TWO_PI, scalar2=None, op0=ALU.mult), ["T1"], ["T2"])
    dve(lambda g: g.tensor_copy(out=TI[:], in_=T2[:]), ["T2"], ["TI"])
    dve(lambda g: g.tensor_copy(out=T2[:], in_=TI[:]), ["TI"], ["T2"])
    dve(lambda g: g.scalar_tensor_tensor(out=T1[:], in0=T2[:], scalar=-TWO_PI, in1=T1[:], op0=ALU.mult, op1=ALU.add),
        ["T2", "T1"], ["T1"])
    dve(lambda g: g.tensor_scalar(out=T2[:], in0=T1[:], scalar1=math.pi, scalar2=None, op0=ALU.is_gt), ["T1"], ["T2"])
    dve(lambda g: g.scalar_tensor_tensor(out=T1[:], in0=T2[:], scalar=-TWO_PI, in1=T1[:], op0=ALU.mult, op1=ALU.add),
        ["T2", "T1"], ["T1"])
    dve(lambda g: g.tensor_scalar(out=T2[:], in0=T1[:], scalar1=-math.pi, scalar2=None, op0=ALU.is_lt), ["T1"], ["T2"])
    dve(lambda g: g.scalar_tensor_tensor(out=T1[:], in0=T2[:], scalar=TWO_PI, in1=T1[:], op0=ALU.mult, op1=ALU.add),
        ["T2", "T1"], ["T1"])
    dve(lambda g: g.tensor_scalar(out=T1[:], in0=T1[:], scalar1=math.pi, scalar2=-math.pi, op0=ALU.min, op1=ALU.max),
        ["T1"], ["T1"])
    act(lambda g: g.activation(out=T1[:], in_=T1[:], func=AF.Sin), ["T1"], ["T1"])
    dve(lambda g: g.tensor_tensor(out=T2[:], in0=bG(adt), in1=bK(mtab), op=ALU.mult), ["adt", "mtab"], ["T2"])
    act(lambda g: g.activation(out=T2[:], in_=T2[:], func=AF.Exp), ["T2"], ["T2"])
    dve(lambda g: g.tensor_tensor(out=TAB[:], in0=T1[:], in1=T2[:], op=ALU.mult), ["T1", "T2"], ["TAB"])
    dve(lambda g: g.tensor_copy(out=EP[:], in_=TAB[:, :, 32:NK]), ["TAB"], ["EP"])
    e1r, e1i = TAB[:, :, 32], TAB[:, :, 32 + NL]
    nr, den, cr, ci, t0, t1_ = sm[0], sm[1], sm[2], sm[3], sm[4], sm[5]
    dve(lambda g: g.tensor_scalar(out=nr[:], in0=e1r, scalar1=-1.0, scalar2=None, op0=ALU.add), ["TAB"], ["nr"])
    dve(lambda g: g.tensor_tensor(out=den[:], in0=lre[:], in1=lre[:], op=ALU.mult), ["lre"], ["den"])
    dve(lambda g: g.tensor_tensor(out=t0[:], in0=lim[:], in1=lim[:], op=ALU.mult), ["lim"], ["t0"])
    dve(lambda g: g.tensor_tensor(out=den[:], in0=den[:], in1=t0[:], op=ALU.add), ["den", "t0"], ["den"])
    dve(lambda g: g.reciprocal(out=den[:], in_=den[:]), ["den"], ["den"])
    dve(lambda g: g.tensor_tensor(out=cr[:], in0=nr[:], in1=lre[:], op=ALU.mult), ["nr", "lre"], ["cr"])
    dve(lambda g: g.tensor_tensor(out=t0[:], in0=e1i, in1=lim[:], op=ALU.mult), ["TAB", "lim"], ["t0"])
    dve(lambda g: g.tensor_tensor(out=cr[:], in0=cr[:], in1=t0[:], op=ALU.add), ["cr", "t0"], ["cr"])
    dve(lambda g: g.tensor_tensor(out=cr[:], in0=cr[:], in1=den[:], op=ALU.mult), ["cr", "den"], ["cr"])
    dve(lambda g: g.tensor_tensor(out=ci[:], in0=e1i, in1=lre[:], op=ALU.mult), ["TAB", "lre"], ["ci"])
    dve(lambda g: g.tensor_tensor(out=t0[:], in0=nr[:], in1=lim[:], op=ALU.mult), ["nr", "lim"], ["t0"])
    dve(lambda g: g.tensor_tensor(out=ci[:], in0=ci[:], in1=t0[:], op=ALU.subtract), ["ci", "t0"], ["ci"])
    dve(lambda g: g.tensor_tensor(out=ci[:], in0=ci[:], in1=den[:], op=ALU.mult), ["ci", "den"], ["ci"])
    b8 = lambda t: V(t, 0, [[1, 32], [0, 8]])
    SAE, SBE = TAB[:, :, 0:8], TAB[:, :, 8:16]

    def cplx(outA, outB, xa, xb, yr, yi, ka, kb):
        dve(lambda g: g.tensor_tensor(out=W8[0][:], in0=xa, in1=yr, op=ALU.mult), ka, ["W80"])
        dve(lambda g: g.tensor_tensor(out=W8[1][:], in0=xb, in1=yi, op=ALU.mult), ka, ["W81"])
        dve(lambda g: g.tensor_tensor(out=outA[:], in0=W8[0][:], in1=W8[1][:], op=ALU.add), ["W80", "W81"], [kb + "A"])
        dve(lambda g: g.tensor_tensor(out=W8[0][:], in0=xb, in1=yr, op=ALU.mult), ka, ["W80"])
        dve(lambda g: g.tensor_tensor(out=W8[1][:], in0=xa, in1=yi, op=ALU.mult), ka, ["W81"])
        dve(lambda g: g.tensor_tensor(out=outB[:], in0=W8[0][:], in1=W8[1][:], op=ALU.subtract), ["W80", "W81"], [kb + "B"])

    cplx(WA, WB, SAE, SBE, b8(cr), b8(ci), ["TAB", "cr", "ci"], "W")
    e8r = V(TAB, 32 + 1, [[NK, 32], [0, 8]])
    e8i = V(TAB, 32 + NL + 1, [[NK, 32], [0, 8]])
    cplx(WA2, WB2, WA[:], WB[:], e8r, e8i, ["TAB", "WA", "WB"], "W2")
    TC1, TC2 = 16, 24

    def outer(dst, w1, o1, m1, w2, o2, m2, sub, gb, kw, kd):
        def wv(w, o):
            if w is TAB:
                return V(TAB, gb * 8 * NK + o, [[NK, 8], [1, 8], [0, 16]])
            return V(w, gb * 8 * 8, [[8, 8], [1, 8], [0, 16]])
        mv = lambda m: V(m, gb * 8 * 16, [[16, 8], [0, 8], [1, 16]])
        d4 = lambda t: V(t, 0, [[128, 8], [16, 8], [1, 16]])
        dve(lambda g: g.tensor_tensor(out=d4(OT[0]), in0=wv(w1, o1), in1=mv(m1), op=ALU.mult), kw, ["OT0"])
        dve(lambda g: g.tensor_tensor(out=d4(OT[1]), in0=wv(w2, o2), in1=mv(m2), op=ALU.mult), kw, ["OT1"])
        dve(lambda g: g.tensor_tensor(out=dst[:], in0=OT[0][:], in1=OT[1][:], op=(ALU.subtract if sub else ALU.add)),
            ["OT0", "OT1"], [kd])

    kw_all = ["TAB", "WAA", "WAB", "W2A", "W2B", "Br", "Bi", "Cr", "Ci"]
    for gb in range(4):
        outer(LBSA, WA, 0, Br, WB, 0, Bi, False, gb, kw_all, "LBSA")
        outer(LBSB, WB, 0, Br, WA, 0, Bi, True, gb, kw_all, "LBSB")
        outer(QS, WA2, 0, Br, WB2, 0, Bi, False, gb, kw_all, "QS")
        outer(RS, TAB, TC1, Cr, TAB, TC2, Ci, False, gb, kw_all, "RS")
        for src, kk, si in ((LBSA, "LBSA", 0), (LBSB, "LBSB", 1)):
            for half in range(2):
                bk = 5 + half
                for g4 in range(4):
                    gl = half * 4 + g4
                    P.add("pe", lambda g, src=src, gl=gl, g4=g4, bk=bk: g.transpose(pb[bk][:, g4 * 128:(g4 + 1) * 128],
                                                                                   src[:, gl, :], identf[:]),
                          reads=[kk, "identf"], writes=[("pb", bk)], silent=(g4 < 3))
                dve(lambda g, si=si, half=half, bk=bk: g.tensor_copy(
                    out=STG[si][:, half * 4:(half + 1) * 4, :], in_=pb[bk][:, :].rearrange("p (a b) -> p a b", a=4)),
                    [("pb", bk)], [("STG", si)])
        for gl in range(8):
            gg = gb * 8 + gl
            P.add("pe", lambda g, gl=gl: g.matmul(pb[4][:, 0:128], lhsT=QS[:, gl, :], rhs=RS[:, gl, :], start=True, stop=True),
                  reads=["QS", "RS"], writes=[("pb", 4)])
            dve(lambda g: g.tensor_tensor(out=KTt[:], in0=pb[4][:, 0:128], in1=maskc[:], op=ALU.mult),
                [("pb", 4), "maskc"], ["KTt"])
            dve(lambda g, gl=gl, gg=gg: g.scalar_tensor_tensor(out=STG[2][:, gl, :], in0=identf[:], scalar=dcol_s[:, gg:gg + 1],
                                                              in1=KTt[:], op0=ALU.mult, op1=ALU.add),
                ["KTt", "identf", "dcol"], [("STG", 2)])
        act(lambda g: g.copy(out=STG[3][:], in_=RS[:]), ["RS"], [("STG", 3)])
        for si in range(4):
            P.add("sp", lambda g, si=si, gb=gb: g.dma_start(out=ssmw[si, :, gb * 1024:(gb + 1) * 1024],
                                                            in_=STG[si][:].rearrange("p a b -> p (a b)")),
                  reads=[("STG", si)], writes=[("ssmw", si)], dma="ssmw%d" % si)
    P.barrier("setup")
    st.close()
    X = sb("X", [128, 8, 1024])
    hT = sb("hT", [128, 8, 1024], BF16)
    U8 = sb("U8", [128, 32, 128], BF16)
    HLA = sb("HLA", [128, 32, 128], BF16)
    xn = [sb("xn%d" % i, [128, 1024], BF16) for i in range(2)]
    junk = sb("junk", [128, 1024], BF16)
    sgt = [sb("sgt%d" % i, [128, 512]) for i in range(2)]
    tmpA = [sb("tmpA%d" % i, [128, 512]) for i in range(2)]

    Xk = [("X", j) for j in range(8)]

    def rstd_from(ss_ap, n_el, key_in, key_out):
        dve(lambda g: g.tensor_scalar(out=ss_ap, in0=ss_ap, scalar1=1.0 / n_el, scalar2=EPS, op0=ALU.mult, op1=ALU.add),
            [key_in], [key_out])
        act(lambda g: g.activation(out=ss_ap, in_=ss_ap, func=AF.Sqrt), [key_out], [key_out])
        dve(lambda g: g.reciprocal(out=ss_ap, in_=ss_ap), [key_out], [key_out])

    def norm_to_hT(n):
        for j in range(8):
            c = rot("ss", 64)
            ssa = sst[:, c:c + 1]
            xb = rot("xn", 2)
            act(lambda g, j=j, ssa=ssa: g.activation(out=junk[:], in_=X[:, j, :], func=AF.Square, accum_out=ssa),
                [("X", j)], ["junk", ("ss", c)])
            rstd_from(ssa, 1024.0, ("ss", c), ("ss", c))
            dve(lambda g, j=j, ssa=ssa, xb=xb: g.tensor_scalar(out=xn[xb][:], in0=X[:, j, :], scalar1=ssa,
                                                               scalar2=None, op0=ALU.mult),
                [("X", j), ("ss", c)], [("xn", xb)])
            for kc in range(8):
                P.add("pe", lambda g, kc=kc, xb=xb: g.transpose(pt[:, kc * 128:(kc + 1) * 128],
                                                                 xn[xb][:, kc * 128:(kc + 1) * 128], identb[:]),
                      reads=[("xn", xb), "identb"], writes=["pt"], silent=(kc < 7))
            for kc in range(8):
                act(lambda g, kc=kc, j=j, n=n: g.activation(out=hT[:, kc, j * 128:(j + 1) * 128],
                                                             in_=pt[:, kc * 128:(kc + 1) * 128], func=AF.Identity,
                                                             scale=Gm[:, n * 8 + kc:n * 8 + kc + 1],
                                                             bias=modT[:, 24 * n + kc:24 * n + kc + 1]),
                    ["pt", "Gm", "modT"], [("hT", j)])

    hT_all = [("hT", j) for j in range(8)]

    def load_G(i):
        P.add("sp", lambda g: g.dma_start(out=Gc[:], in_=gts[i]), reads=[("gts", i)], writes=["Gc"], dma="gc")

    def ffn(fi, actT, wds):
        wgr, wur, wdr = wg[fi], wu[fi], wdn[fi]
        k8 = lambda ap: ap.rearrange("p (k n) -> p k n", k=8)
        for (f0, nf) in ((0, 8), (8, 8), (16, 6)):
            for w0_ in range(0, nf, 4):
                wn_ = min(4, nf - w0_)
                cast_load(wds[:, w0_:w0_ + wn_, :], wdr[:, (f0 + w0_) * 1024:(f0 + w0_ + wn_) * 1024].rearrange("p (a b) -> p a b", a=wn_), "wds", wn_, 1024)
            for q0 in range(0, nf, 4):
                nq = min(4, nf - q0)
                c0 = (f0 + q0) * 128
                sg_ = rot("ring", 3)
                cast_load(ring[sg_][:, :, 0:nq * 128], k8(wgr[:, 8 * c0:8 * (c0 + nq * 128)]), ("ring", sg_), 8, nq * 128)
                su_ = rot("ring", 3)
                cast_load(ring[su_][:, :, 0:nq * 128], k8(wur[:, 8 * c0:8 * (c0 + nq * 128)]), ("ring", su_), 8, nq * 128)
                for fq in range(nq):
                    fl = q0 + fq
                    for nb in range(2):
                        pr = rot("gu", 2)
                        bg, bu = 2 * pr, 2 * pr + 1
                        for kc in range(8):
                            P.add("pe", lambda g, kc=kc, fq=fq, nb=nb, bg=bg, sg_=sg_: g.matmul(
                                pb[bg][:, :], lhsT=ring[sg_][:, kc, fq * 128:(fq + 1) * 128],
                                rhs=hT[:, kc, nb * 512:(nb + 1) * 512], start=(kc == 0), stop=(kc == 7)),
                                reads=[("ring", sg_)] + hT_all, writes=[("pb", bg)], silent=(kc < 7))
                        for kc in range(8):
                            P.add("pe", lambda g, kc=kc, fq=fq, nb=nb, bu=bu, su_=su_: g.matmul(
                                pb[bu][:, :], lhsT=ring[su_][:, kc, fq * 128:(fq + 1) * 128],
                                rhs=hT[:, kc, nb * 512:(nb + 1) * 512], start=(kc == 0), stop=(kc == 7)),
                                reads=[("ring", su_)] + hT_all, writes=[("pb", bu)], silent=(kc < 7))
                        si = rot("sg", 2)
                        act(lambda g, si=si, bg=bg: g.activation(out=sgt[si][:], in_=pb[bg][:, :], func=AF.Silu),
                            [("pb", bg)], [("sgt", si)])
                        dve(lambda g, si=si, bu=bu, fl=fl, nb=nb: g.tensor_tensor(
                            out=actT[:, fl, nb * 512:(nb + 1) * 512], in0=sgt[si][:], in1=pb[bu][:, :], op=ALU.mult),
                            [("sgt", si), ("pb", bu)], [("actT", fl)])
            for j in range(8):
                for nh in range(2):
                    bk = 4 + rot("dn", 2)
                    for fl in range(nf):
                        P.add("pe", lambda g, fl=fl, j=j, nh=nh, bk=bk, nf=nf: g.matmul(
                            pb[bk][:, :], lhsT=actT[:, fl, j * 128:(j + 1) * 128],
                            rhs=wds[:, fl, nh * 512:(nh + 1) * 512], start=(fl == 0), stop=(fl == nf - 1)),
                            reads=[("actT", fl), "wds"], writes=[("pb", bk)], silent=(fl < nf - 1))
                    ti = rot("ta", 2)
                    dve(lambda g, ti=ti, bk=bk, nh=nh: g.tensor_tensor(out=tmpA[ti][:], in0=pb[bk][:, :],
                                                                      in1=Gc[:, nh * 512:(nh + 1) * 512], op=ALU.mult),
                        [("pb", bk), "Gc"], [("tmpA", ti)])
                    dve(lambda g, ti=ti, j=j, nh=nh: g.tensor_tensor(
                        out=X[:, j, nh * 512:(nh + 1) * 512], in0=X[:, j, nh * 512:(nh + 1) * 512],
                        in1=tmpA[ti][:], op=ALU.add),
                        [("tmpA", ti), ("X", j)], [("X", j)])

    wig = lambda gi: w_in[:, gi * 4096:(gi + 1) * 4096].rearrange("p (k n) -> p k n", k=8)
    flat3 = lambda t: t[:].rearrange("p a b -> p (a b)")

    load_G(0)

    def phase1(h):
        s1 = ExitStack()
        actT = sb("actT", [128, 8, 1024], BF16, s1)
        wds = sb("wds", [128, 8, 1024], BF16, s1)
        P.add("sp", lambda g, h=h: g.dma_start(out=X[:], in_=(xpr[:, h] if h < 2 else xr[:, h - 2])), writes=Xk, dma="x")
        norm_to_hT(0)
        ffn(0, actT, wds)
        P.add("sp", lambda g, h=h: g.dma_start(out=xs[h], in_=X[:].rearrange("p a b -> p (a b)")),
              reads=Xk, writes=[("xs", h)], dma="xs")
        norm_to_hT(1)
        P.barrier("p1a%d" % h)
        s1.close()
        s1 = ExitStack()
        Utok = sb("Utok", [128, 32, 8, 16], BF16, s1)
        swA = sb("swA", [128, 32, 128], BF16, s1); swB = sb("swB", [128, 32, 128], BF16, s1)
        SAb = sb("SAb", [128, 32, 128], F32, s1); SBb = sb("SBb", [128, 32, 128], F32, s1)
        sc = [sb("sc%d" % i, [128, 32, 8], F32, s1) for i in range(4)]
        ctmp = sb("ctmp", [128, 2], F32, s1)
        su = ring_load(wig(3), None)
        for j in range(8):
            for kc in range(8):
                P.add("pe", lambda g, kc=kc, j=j: g.matmul(pb[0][:, :], lhsT=hT[:, kc, j * 128:(j + 1) * 128],
                                                           rhs=ring[su][:, kc, :], start=(kc == 0), stop=(kc == 7)),
                      reads=[("ring", su), ("hT", j)], writes=[("pb", 0)], silent=(kc < 7))
            act(lambda g, j=j: g.copy(out=Utok[:, :, j, :], in_=pb[0][:, :].rearrange("p (a b) -> p a b", a=32)),
                [("pb", 0)], ["Utok"])
        for gb in range(4):
            for g8 in range(8):
                gg = gb * 8 + g8
                P.add("pe", lambda g, gg=gg, g8=g8: g.transpose(pt[:, g8 * 128:(g8 + 1) * 128],
                                                               Utok[:, gg].rearrange("p a b -> p (a b)"), identb[:]),
                      reads=["Utok", "identb"], writes=["pt"], silent=(g8 < 7))
            dve(lambda g, gb=gb: g.tensor_copy(out=U8[:, gb * 8:(gb + 1) * 8, :],
                                               in_=pt[:, :].rearrange("p (a b) -> p a b", a=8)),
                ["pt"], ["U8"])
        P.add("sp", lambda g, h=h: g.dma_start(out=u8s[h], in_=flat3(U8)), reads=["U8"], writes=[("u8s", h)], dma="u8s")
        if h == 1:
            scv = ring_load(wig(1), None)
            svv = ring_load(wig(2), None)
            halo_rhs = lambda kc: V(hT, kc * 1024 + 6 * 128 + 127, [[128, 2]])
            for cc in range(4):
                for kc in range(8):
                    P.add("pe", lambda g, kc=kc, cc=cc: g.matmul(pb[1][:, 0:2], lhsT=ring[scv][:, kc, cc * 128:(cc + 1) * 128],
                                                                 rhs=halo_rhs(kc), start=(kc == 0), stop=(kc == 7)),
                          reads=[("ring", scv)] + hT_all, writes=[("pb", 1)], silent=(kc < 7))
                for kc in range(8):
                    P.add("pe", lambda g, kc=kc, cc=cc: g.matmul(pb[2][:, 0:2], lhsT=ring[svv][:, kc, cc * 128:(cc + 1) * 128],
                                                                 rhs=halo_rhs(kc), start=(kc == 0), stop=(kc == 7)),
                          reads=[("ring", svv)] + hT_all, writes=[("pb", 2)], silent=(kc < 7))
                act(lambda g: g.copy(out=ctmp[:], in_=pb[1][:, 0:2]), [("pb", 1)], ["ctmp"])
                dve(lambda g, cc=cc: g.tensor_tensor(out=PAY[:, 64 + cc * 2:66 + cc * 2], in0=ctmp[:], in1=pb[2][:, 0:2],
                                                     op=ALU.mult), ["ctmp", ("pb", 2)], ["PAY"])
        for gh in range(1):
            for (swt, si) in ((swA, 0), (swB, 1)):
                P.add("sp", lambda g, swt=swt, si=si, gh=gh: g.dma_start(out=flat3(swt), in_=ssmw[si]),
                      reads=[("ssmw", si)], writes=[("sw", si)], dma="sw%d" % si)
            for (dstb, swt, si, kk) in ((SAb, swA, 0, "SAb"), (SBb, swB, 1, "SBb")):
                for gq in range(8):
                    bk = 3 + (gq % 2)
                    for g4 in range(4):
                        gl = gq * 4 + g4
                        P.add("pe", lambda g, gl=gl, g4=g4, bk=bk, swt=swt, gh=gh: g.matmul(
                            pb[bk][:, g4 * 128:(g4 + 1) * 128], lhsT=swt[:, gl, :], rhs=U8[:, gl, :],
                            start=True, stop=True),
                            reads=[("sw", si), "U8"], writes=[("pb", bk)], silent=(g4 < 3))
                    act(lambda g, dstb=dstb, gq=gq, bk=bk: g.copy(out=dstb[:, gq * 4:(gq + 1) * 4, :],
                                                                 in_=pb[bk][:, :].rearrange("p (a b) -> p a b", a=4)),
                        [("pb", bk)], [kk])
            vm = lambda t, m: V(t, m, [[128, 32], [16, 8]])
            hv = lambda m, gh=gh: V(HLA, m, [[128, 32], [16, 8]])
            Er = V(EP, 2, [[2 * NL, 32], [0, 8]])
            Ei = V(EP, NL + 2, [[2 * NL, 32], [0, 8]])
            dve(lambda g, hv=hv: g.memset(hv(0), 0.0), [], ["HLA"])
            for m in range(1, 16):
                a_c, b_c = vm(SAb, m - 1), vm(SBb, m - 1)
                dve(lambda g, m=m, a_c=a_c, hv=hv: g.tensor_copy(out=hv(m), in_=a_c), ["SAb"], ["HLA"])
                dve(lambda g, a_c=a_c, Er=Er: g.tensor_tensor(out=sc[0][:], in0=a_c, in1=Er, op=ALU.mult), ["SAb", "EP"], ["sc0"])
                dve(lambda g, b_c=b_c, Ei=Ei: g.tensor_tensor(out=sc[1][:], in0=b_c, in1=Ei, op=ALU.mult), ["SBb", "EP"], ["sc1"])
                dve(lambda g: g.tensor_tensor(out=sc[0][:], in0=sc[0][:], in1=sc[1][:], op=ALU.add), ["sc0", "sc1"], ["sc0"])
                dve(lambda g, b_c=b_c, Er=Er: g.tensor_tensor(out=sc[2][:], in0=b_c, in1=Er, op=ALU.mult), ["SBb", "EP"], ["sc2"])
                dve(lambda g, a_c=a_c, Ei=Ei: g.tensor_tensor(out=sc[3][:], in0=a_c, in1=Ei, op=ALU.mult), ["SAb", "EP"], ["sc3"])
                dve(lambda g: g.tensor_tensor(out=sc[2][:], in0=sc[2][:], in1=sc[3][:], op=ALU.subtract), ["sc2", "sc3"], ["sc2"])
                dve(lambda g, m=m: g.tensor_tensor(out=vm(SAb, m), in0=vm(SAb, m), in1=sc[0][:], op=ALU.add),
                    ["SAb", "sc0"], ["SAb"])
                dve(lambda g, m=m: g.tensor_tensor(out=vm(SBb, m), in0=vm(SBb, m), in1=sc[2][:], op=ALU.add),
                    ["SBb", "sc2"], ["SBb"])
            dve(lambda g, gh=gh, h=h: g.tensor_copy(out=HEA[:, :, h * 8:(h + 1) * 8], in_=vm(SAb, 15)),
                ["SAb"], ["HEA"])
            dve(lambda g, gh=gh, h=h: g.tensor_copy(out=HEB[:, :, h * 8:(h + 1) * 8], in_=vm(SBb, 15)),
                ["SBb"], ["HEB"])
        P.add("sp", lambda g, h=h: g.dma_start(out=hlas[h], in_=flat3(HLA)), reads=["HLA"], writes=[("hlas", h)], dma="hlas")
        P.barrier("p1b%d" % h)
        s1.close()

    phase1(0)
    phase1(1)
    phase1(2)
    phase1(3)

    s2 = ExitStack()
    cs = [sb("cs%d" % i, [128, 32], F32, s2) for i in range(4)]
    E128r, E128i = EP[:, :, 3], EP[:, :, NL + 3]
    flag = sel_s[:, 0:1]
    dve(lambda g: g.tensor_scalar(out=HEA[:, :, 0:16], in0=HEA[:, :, 0:16], scalar1=flag, scalar2=None, op0=ALU.mult),
        ["HEA", "sel"], ["HEA"])
    dve(lambda g: g.tensor_scalar(out=HEB[:, :, 0:16], in0=HEB[:, :, 0:16], scalar1=flag, scalar2=None, op0=ALU.mult),
        ["HEB", "sel"], ["HEB"])
    dve(lambda g: g.tensor_scalar(out=HIN[:, 64:72], in0=PAY[:, 64:72], scalar1=flag, scalar2=None, op0=ALU.mult), ["PAY", "sel"], ["HIN"])
    dve(lambda g: g.memset(HIA[:, :, 0], 0.0), [], ["HIA"])
    dve(lambda g: g.memset(HIB[:, :, 0], 0.0), [], ["HIB"])
    for ck in range(32):
        oa, ob = HIA[:, :, ck + 1], HIB[:, :, ck + 1]
        ia, ib = HIA[:, :, ck], HIB[:, :, ck]
        dve(lambda g, ia=ia: g.tensor_tensor(out=cs[0][:], in0=ia, in1=E128r, op=ALU.mult), ["HIA", "EP"], ["cs0"])
        dve(lambda g, ib=ib: g.tensor_tensor(out=cs[1][:], in0=ib, in1=E128i, op=ALU.mult), ["HIB", "EP"], ["cs1"])
        dve(lambda g: g.tensor_tensor(out=cs[0][:], in0=cs[0][:], in1=cs[1][:], op=ALU.add), ["cs0", "cs1"], ["cs0"])
        dve(lambda g, ib=ib: g.tensor_tensor(out=cs[2][:], in0=ib, in1=E128r, op=ALU.mult), ["HIB", "EP"], ["cs2"])
        dve(lambda g, ia=ia: g.tensor_tensor(out=cs[3][:], in0=ia, in1=E128i, op=ALU.mult), ["HIA", "EP"], ["cs3"])
        dve(lambda g: g.tensor_tensor(out=cs[2][:], in0=cs[2][:], in1=cs[3][:], op=ALU.subtract), ["cs2", "cs3"], ["cs2"])
        dve(lambda g, oa=oa, ck=ck: g.tensor_tensor(out=oa, in0=cs[0][:], in1=HEA[:, :, ck], op=ALU.add),
            ["cs0", "HEA"], ["HIA"])
        dve(lambda g, ob=ob, ck=ck: g.tensor_tensor(out=ob, in0=cs[2][:], in1=HEB[:, :, ck], op=ALU.add),
            ["cs2", "HEB"], ["HIB"])
    P.barrier("xchg")
    s2.close()

    yr_ = y.rearrange("(h s j) d -> s h j d", h=2, s=128, j=8)
    def phase3(h):
        s3o = ExitStack()
        ycb = sb("ycb", [128, 4, 1024], BF16, s3o)
        YG = sb("YG", [128, 8, 512], BF16, s3o)
        s3 = ExitStack()
        sw2 = sb("sw2", [128, 2, 16, 128], BF16, s3)
        zT = sb("zT", [128, 4, 8, 129], BF16, s3)
        Hbf = sb("Hbf", [128, 16, 128], BF16, s3)
        cv = sb("cv", [128, 8, 128], F32, s3)
        ycf = sb("ycf", [128, 512], F32, s3)
        sqf = sb("sqf", [128, 512], F32, s3)
        rb = sb("rb", [128, 1024], F32, s3)
        fw = [sb("fw%d" % i, [128, 8, 8, 16], F32, s3) for i in range(2)]
        P.add("sp", lambda g, h=h: g.dma_start(out=X[:].rearrange("p a b -> p (a b)"), in_=xs[h + 2]),
              reads=[("xs", h + 2)], writes=Xk, dma="x")
        P.add("sp", lambda g, h=h: g.dma_start(out=flat3(U8), in_=u8s[h + 2]), reads=[("u8s", h + 2)], writes=["U8"], dma="u8l")
        P.add("sp", lambda g, h=h: g.dma_start(out=flat3(HLA), in_=hlas[h + 2]), reads=[("hlas", h + 2)], writes=["HLA"], dma="hlal")
        load_G(1)
        norm_to_hT(1)
        sC = ring_load(wig(1), None)
        sV = ring_load(wig(2), None)
        sB = ring_load(wig(0), None)
        dve(lambda g: g.memset(zT[:, :, :, 0:1], 0.0), [], ["zT"])
        if h == 0:
            dve(lambda g: g.tensor_copy(out=zT[:, :, 6:8, 0], in_=HIN[:, 64:72].rearrange("p (a b) -> p a b", a=4)),
                ["HIN", "zT"], ["zT"])
        else:
            dve(lambda g: g.tensor_copy(out=zT[:, :, :, 0], in_=zlast[:]), ["zlast", "zT"], ["zT"])
        for cc in range(4):
            for nb in range(2):
                for (slot, bk) in ((sC, 0), (sV, 1)):
                    for kc in range(8):
                        P.add("pe", lambda g, kc=kc, cc=cc, nb=nb, slot=slot, bk=bk: g.matmul(
                            pb[bk][:, :], lhsT=ring[slot][:, kc, cc * 128:(cc + 1) * 128],
                            rhs=hT[:, kc, nb * 512:(nb + 1) * 512], start=(kc == 0), stop=(kc == 7)),
                            reads=[("ring", slot)] + hT_all, writes=[("pb", bk)], silent=(kc < 7))
                act(lambda g: g.copy(out=sqf[:], in_=pb[0][:, :]), [("pb", 0)], ["sqf"])
                dve(lambda g, cc=cc, nb=nb: g.tensor_tensor(
                    out=zT[:, cc, nb * 4:(nb + 1) * 4, 1:129], in0=sqf[:].rearrange("p (a b) -> p a b", a=4),
                    in1=pb[1][:, :].rearrange("p (a b) -> p a b", a=4), op=ALU.mult),
                    ["sqf", ("pb", 1)], ["zT"])
            w0, w1, w2 = (convw_s[:, k * 4 + cc:k * 4 + cc + 1] for k in range(3))
            for (js, a2, a1, a0) in (
                    (slice(2, 8), zT[:, cc, 2:8, 1:129], zT[:, cc, 1:7, 1:129], zT[:, cc, 0:6, 1:129]),
                    (slice(1, 2), zT[:, cc, 1:2, 1:129], zT[:, cc, 0:1, 1:129], zT[:, cc, 7:8, 0:128]),
                    (slice(0, 1), zT[:, cc, 0:1, 1:129], zT[:, cc, 7:8, 0:128], zT[:, cc, 6:7, 0:128])):
                dve(lambda g, js=js, a2=a2, w2=w2: g.tensor_scalar(out=cv[:, js, :], in0=a2, scalar1=w2, scalar2=None,
                                                                   op0=ALU.mult), ["zT", "convw"], ["cv"])
                dve(lambda g, js=js, a1=a1, w1=w1: g.scalar_tensor_tensor(out=cv[:, js, :], in0=a1, scalar=w1, in1=cv[:, js, :],
                                                                          op0=ALU.mult, op1=ALU.add), ["zT", "convw", "cv"], ["cv"])
                dve(lambda g, js=js, a0=a0, w0=w0: g.scalar_tensor_tensor(out=cv[:, js, :], in0=a0, scalar=w0, in1=cv[:, js, :],
                                                                          op0=ALU.mult, op1=ALU.add), ["zT", "convw", "cv"], ["cv"])
            for nb in range(2):
                for kc in range(8):
                    P.add("pe", lambda g, kc=kc, cc=cc, nb=nb: g.matmul(
                        pb[2][:, :], lhsT=ring[sB][:, kc, cc * 128:(cc + 1) * 128],
                        rhs=hT[:, kc, nb * 512:(nb + 1) * 512], start=(kc == 0), stop=(kc == 7)),
                        reads=[("ring", sB)] + hT_all, writes=[("pb", 2)], silent=(kc < 7))
                dve(lambda g, nb=nb: g.tensor_tensor(out=ycf[:], in0=pb[2][:, :],
                                                     in1=cv[:, nb * 4:(nb + 1) * 4, :].rearrange("p a b -> p (a b)"), op=ALU.mult),
                    [("pb", 2), "cv"], ["ycf"])
                act(lambda g: g.activation(out=sqf[:], in_=ycf[:], func=AF.Square), ["ycf"], ["sqf"])
                P.add("pe", lambda g, cc=cc, nb=nb: g.matmul(pb[5 + nb][:, :], lhsT=onesf[:], rhs=sqf[:],
                                                             start=(cc == 0), stop=(cc == 3)),
                      reads=["sqf", "onesf"], writes=[("pb", 5 + nb)])
                act(lambda g, cc=cc, nb=nb: g.copy(out=ycb[:, cc, nb * 512:(nb + 1) * 512], in_=ycf[:]), ["ycf"], ["ycb"])
        if h == 0:
            dve(lambda g: g.tensor_copy(out=zlast[:], in_=zT[:, :, :, 128]), ["zT"], ["zlast"])
        for nb in range(2):
            dve(lambda g, nb=nb: g.tensor_scalar(out=rb[:, nb * 512:(nb + 1) * 512], in0=pb[5 + nb][:, :], scalar1=1.0 / 512,
                                                 scalar2=EPS, op0=ALU.mult, op1=ALU.add), [("pb", 5 + nb)], ["rb"])
        act(lambda g: g.activation(out=rb[:], in_=rb[:], func=AF.Sqrt), ["rb"], ["rb"])
        dve(lambda g: g.reciprocal(out=rb[:], in_=rb[:]), ["rb"], ["rb"])
        for cc in range(4):
            dve(lambda g, cc=cc: g.scalar_tensor_tensor(out=ycb[:, cc, :], in0=ycb[:, cc, :], scalar=onc_s[:, cc:cc + 1],
                                                        in1=rb[:], op0=ALU.mult, op1=ALU.mult), ["ycb", "rb", "onc"], ["ycb"])
        for gh in range(2):
            for si in range(2):
                P.add("sp", lambda g, si=si, gh=gh: g.dma_start(out=sw2[:, si].rearrange("p a b -> p (a b)"),
                                                                in_=ssmw[2 + si, :, gh * 2048:(gh + 1) * 2048]),
                      reads=[("ssmw", 2 + si)], writes=[("sw2", si)], dma="sw2%d" % si)
            for q in range(2):
                g0 = gh * 16 + q * 8
                Er8 = V(EP, g0 * 2 * NL + 4, [[2 * NL, 8], [0, 8], [1, 16]])
                Ei8 = V(EP, g0 * 2 * NL + NL + 4, [[2 * NL, 8], [0, 8], [1, 16]])
                hA = V(HIA, g0 * 33 + 16 + h * 8, [[33, 8], [1, 8], [0, 16]])
                hB = V(HIB, g0 * 33 + 16 + h * 8, [[33, 8], [1, 8], [0, 16]])
                dve(lambda g, Er8=Er8, hA=hA: g.tensor_tensor(out=fw[0][:], in0=Er8, in1=hA, op=ALU.mult), ["EP", "HIA"], ["fw0"])
                dve(lambda g, Ei8=Ei8, hB=hB: g.tensor_tensor(out=fw[1][:], in0=Ei8, in1=hB, op=ALU.mult), ["EP", "HIB"], ["fw1"])
                dve(lambda g: g.tensor_tensor(out=fw[0][:], in0=fw[0][:], in1=fw[1][:], op=ALU.add), ["fw0", "fw1"], ["fw0"])
                dve(lambda g, g0=g0, q=q: g.tensor_tensor(out=Hbf[:, q * 8:(q + 1) * 8, :], in0=HLA[:, g0:g0 + 8, :],
                                                          in1=fw[0][:].rearrange("p a b c -> p a (b c)"), op=ALU.add),
                    ["fw0", "HLA"], ["Hbf"])
            for gq in range(4):
                bk = 3 + (gq % 2)
                for g4 in range(4):
                    gl = gq * 4 + g4
                    P.add("pe", lambda g, gl=gl, g4=g4, bk=bk, gh=gh: g.matmul(
                        pb[bk][:, g4 * 128:(g4 + 1) * 128], lhsT=U8[:, gh * 16 + gl, :], rhs=sw2[:, 0, gl, :],
                        start=True, stop=False), reads=[("sw2", 0), "U8"], writes=[("pb", bk)], silent=True)
                    P.add("pe", lambda g, gl=gl, g4=g4, bk=bk: g.matmul(
                        pb[bk][:, g4 * 128:(g4 + 1) * 128], lhsT=Hbf[:, gl, :], rhs=sw2[:, 1, gl, :],
                        start=False, stop=True), reads=[("sw2", 1), "Hbf"], writes=[("pb", bk)], silent=(g4 < 3))
                act(lambda g, bk=bk: g.activation(out=sqf[:], in_=pb[bk][:, :], func=AF.Square), [("pb", bk)], ["sqf"])
                dve(lambda g: g.tensor_scalar(out=sqf[:], in0=sqf[:], scalar1=0.044715, scalar2=1.0, op0=ALU.mult, op1=ALU.add),
                    ["sqf"], ["sqf"])
                dve(lambda g, bk=bk: g.tensor_tensor(out=sqf[:], in0=sqf[:], in1=pb[bk][:, :], op=ALU.mult),
                    ["sqf", ("pb", bk)], ["sqf"])
                act(lambda g: g.activation(out=sqf[:], in_=sqf[:], func=AF.Sigmoid, scale=1.5957691216057308), ["sqf"], ["sqf"])
                gq_abs = gh * 4 + gq
                dve(lambda g, bk=bk, gq_abs=gq_abs: g.tensor_tensor(
                    out=V(YG, gq_abs * 64, [[16, 4], [512, 8], [1, 16]]), in0=V(sqf, 0, [[128, 4], [16, 8], [1, 16]]),
                    in1=V(pb[bk], 0, [[128, 4], [16, 8], [1, 16]]), op=ALU.mult), ["sqf", ("pb", bk)], ["YG"])
        P.barrier("p3a%d" % h)
        s3.close()
        s3 = ExitStack()
        wouts = sb("wouts", [128, 8, 1024], BF16, s3)
        gluws = sb("gluws", [128, 4, 512], BF16, s3)
        ygT = [sb("ygT%d" % i, [128, 4, 128], BF16, s3) for i in range(2)]
        ysT = [sb("ysT%d" % i, [128, 4, 128], BF16, s3) for i in range(2)]
        yss = sb("yss", [128, 512], F32, s3)
        ysn = sb("ysn", [128, 512], BF16, s3)
        sq2 = sb("sq2", [128, 512], F32, s3)
        wor_ = wout.rearrange("p (k n) -> p k n", k=8)
        cast_load(wouts[:, 0:4, :], wor_[:, 0:4, :], "wouts", 4, 1024)
        cast_load(wouts[:, 4:8, :], wor_[:, 4:8, :], "wouts", 4, 1024)
        cast_load(gluws[:], gluw.rearrange("p (k n) -> p k n", k=4), "gluws", 4, 512)
        for j in range(8):
            gi = j % 2
            for cc in range(4):
                P.add("pe", lambda g, cc=cc, j=j: g.transpose(pt[:, cc * 128:(cc + 1) * 128], YG[:, j, cc * 128:(cc + 1) * 128],
                                                              identb[:]), reads=["YG", "identb"], writes=["pt"], silent=(cc < 3))
            act(lambda g, gi=gi: g.copy(out=ygT[gi][:], in_=pt[:, 0:512].rearrange("p (a b) -> p a b", a=4)),
                ["pt"], [("ygT", gi)])
            for cc in range(4):
                P.add("pe", lambda g, cc=cc, gi=gi: g.matmul(pb[0][:, :], lhsT=ygT[gi][:, cc, :], rhs=gluws[:, cc, :],
                                                             start=(cc == 0), stop=False),
                      reads=[("ygT", gi), "gluws"], writes=[("pb", 0)], silent=True)
            P.add("pe", lambda g: g.matmul(pb[0][:, :], lhsT=onesb[0:1, :], rhs=glub_b[0:1, :], start=False, stop=True),
                  reads=["onesb", "glubb"], writes=[("pb", 0)])
            act(lambda g: g.activation(out=sq2[:], in_=pb[0][:, :], func=AF.Sigmoid), [("pb", 0)], ["sq2"])
            dve(lambda g, j=j: g.tensor_tensor(out=yss[:], in0=YG[:, j, :], in1=sq2[:], op=ALU.mult), ["YG", "sq2"], ["yss"])
            c = rot("ss", 64)
            ssa = sst[:, c:c + 1]
            act(lambda g, ssa=ssa: g.activation(out=junk[:, 0:512], in_=yss[:], func=AF.Square, accum_out=ssa),
                ["yss"], ["junk", ("ss", c)])
            rstd_from(ssa, 512.0, ("ss", c), ("ss", c))
            dve(lambda g, ssa=ssa: g.tensor_scalar(out=ysn[:], in0=yss[:], scalar1=ssa, scalar2=None, op0=ALU.mult),
                ["yss", ("ss", c)], ["ysn"])
            for cc in range(4):
                P.add("pe", lambda g, cc=cc: g.transpose(pt[:, cc * 128:(cc + 1) * 128], ysn[:, cc * 128:(cc + 1) * 128], identb[:]),
                      reads=["ysn", "identb"], writes=["pt"], silent=(cc < 3))
            for cc in range(4):
                act(lambda g, cc=cc, gi=gi: g.activation(out=ysT[gi][:, cc, :], in_=pt[:, cc * 128:(cc + 1) * 128],
                                                         func=AF.Copy, scale=ons_s[:, cc:cc + 1]),
                    ["pt", "ons"], [("ysT", gi)])
            for nh in range(2):
                bk = 1 + nh
                for kc in range(8):
                    lh = ycb[:, kc, j * 128:(j + 1) * 128] if kc < 4 else ysT[gi][:, kc - 4, :]
                    P.add("pe", lambda g, kc=kc, nh=nh, bk=bk, lh=lh: g.matmul(
                        pb[bk][:, :], lhsT=lh, rhs=wouts[:, kc, nh * 512:(nh + 1) * 512], start=(kc == 0), stop=(kc == 7)),
                        reads=["ycb", ("ysT", gi), "wouts"], writes=[("pb", bk)], silent=(kc < 7))
                ti = rot("ta", 2)
                dve(lambda g, ti=ti, bk=bk, nh=nh: g.tensor_tensor(out=tmpA[ti][:], in0=pb[bk][:, :],
                                                                  in1=Gc[:, nh * 512:(nh + 1) * 512], op=ALU.mult),
                    [("pb", bk), "Gc"], [("tmpA", ti)])
                dve(lambda g, ti=ti, j=j, nh=nh: g.tensor_tensor(
                    out=X[:, j, nh * 512:(nh + 1) * 512], in0=X[:, j, nh * 512:(nh + 1) * 512],
                    in1=tmpA[ti][:], op=ALU.add), [("tmpA", ti), ("X", j)], [("X", j)])
        P.barrier("p3b%d" % h)
        s3.close()
        s3o.close()
        s4 = ExitStack()
        actT = sb("actT2", [128, 8, 1024], BF16, s4)
        wds = sb("wds2", [128, 8, 1024], BF16, s4)
        ot = [sb("ot%d" % i, [128, 1024], F32, s4) for i in range(2)]
        FG = sb("FG", [128, 1024], F32, s4)
        P.add("sp", lambda g: g.dma_start(out=FG[:], in_=bass.AP(fin.tensor, fin.offset, [[0, 128], [1, 1024]])),
              writes=["FG"], dma="fg")
        load_G(2)
        norm_to_hT(2)
        ffn(1, actT, wds)
        for j in range(8):
            tile = h * 8 + j
            c = rot("ss", 64)
            ssa = sst[:, c:c + 1]
            oi = j % 2
            act(lambda g, j=j, ssa=ssa: g.activation(out=junk[:], in_=X[:, j, :], func=AF.Square, accum_out=ssa),
                [("X", j)], ["junk", ("ss", c)])
            rstd_from(ssa, 1024.0, ("ss", c), ("ss", c))
            dve(lambda g, j=j, ssa=ssa, oi=oi: g.scalar_tensor_tensor(out=ot[oi][:], in0=X[:, j, :], scalar=ssa,
                                                                      in1=FG[:], op0=ALU.mult, op1=ALU.mult),
                [("X", j), ("ss", c), "FG"], [("ot", oi)])
            P.add("sp", lambda g, oi=oi, j=j, h=h: g.dma_start(out=yr_[:, h, j, :], in_=ot[oi][:]),
                  reads=[("ot", oi)], writes=[("y", tile)], dma="y%d" % oi)
        P.barrier("p3c%d" % h)
        s4.close()

    phase3(0)
    phase3(1)
    P.add("sp", lambda g: g.nop(), reads=[("y", t) for t in range(16)], writes=["done"], silent=True)
    block = es.enter_context(nc.Block())
    P.finalize(block)
    es.close()
    return nc


_NC = None


def _consts():
    identf = np.eye(128, dtype=np.float32)
    ii = np.arange(128) // 16
    maskc = (ii[None, :] >= ii[:, None]).astype(np.float32)
    nl = len(MLIST)
    mtab = np.zeros((128, NK), np.float32)
    ptab = np.zeros((128, NK), np.float32)
    lo = np.arange(128) < 64
    m70 = np.arange(7, -1, -1, dtype=np.float32)
    m18 = np.arange(1, 9, dtype=np.float32)
    mtab[:, 0:8] = m70; mtab[:, 8:16] = m70; mtab[:, 16:24] = m18; mtab[:, 24:32] = m18
    mtab[:, 32:32 + nl] = np.array(MLIST, np.float32); mtab[:, 32 + nl:] = np.array(MLIST, np.float32)
    hp = math.pi / 2
    ptab[:, 0:8] = np.where(lo, hp, 0.0)[:, None]
    ptab[:, 8:16] = np.where(lo, math.pi, hp)[:, None]
    ptab[:, 16:24] = np.where(lo, hp, math.pi)[:, None]
    ptab[:, 24:32] = np.where(lo, math.pi, -hp)[:, None]
    ptab[:, 32:32 + nl] = hp
    ptab[:, 32 + nl:] = 0.0
    return identf, maskc, mtab, ptab


def _lay_kn(w, groups):
    w = np.asarray(w, dtype=np.float32)
    kc = w.shape[0] // 128
    w3 = w.reshape(kc, 128, w.shape[1]).transpose(1, 0, 2)
    return np.ascontiguousarray(np.concatenate([w3[:, :, c0:c0 + n].reshape(128, -1) for (c0, n) in groups], axis=1))


_FG = [(0, 512), (512, 512), (1024, 512), (1536, 512), (2048, 512), (2560, 256)]


def _lay_rows(w):
    w = np.asarray(w, dtype=np.float32)
    fc = w.shape[0] // 128
    return np.ascontiguousarray(w.reshape(fc, 128, w.shape[1]).transpose(1, 0, 2).reshape(128, -1))


def make_in_maps(inp, ncores=8):
    f = lambda a: np.ascontiguousarray(np.asarray(a, dtype=np.float32))
    x = f(inp["x"]); cond = f(inp["cond"])
    col8 = lambda v: f(np.asarray(v).reshape(8, 128).T)
    col4 = lambda v: f(np.asarray(v).reshape(4, 128).T)
    rep = lambda a: f(np.concatenate([a, a], axis=0))
    identf, maskc, mtab, ptab = _consts()
    lamre = rep(np.asarray(inp["lambda_re"])[0].T); lamim = rep(np.asarray(inp["lambda_im"])[0].T)
    logdt = f(np.broadcast_to(np.asarray(inp["log_dt"])[0][None, :], (128, 32)))
    bre = rep(np.asarray(inp["ssm_b_re"])[0].transpose(1, 0, 2).reshape(64, 512))
    bim = rep(np.asarray(inp["ssm_b_im"])[0].transpose(1, 0, 2).reshape(64, 512))
    cre = rep(np.asarray(inp["ssm_c_re"])[0].transpose(2, 0, 1).reshape(64, 512))
    cim = rep(np.asarray(inp["ssm_c_im"])[0].transpose(2, 0, 1).reshape(64, 512))
    d = np.asarray(inp["ssm_d"])[0].reshape(32, 16)
    dcol = f(np.tile(d.T, (8, 1)))
    convw = np.asarray(inp["conv_w"])[0]
    convw_l = f(np.concatenate([convw[k].reshape(4, 128).T for k in range(3)], axis=1))
    shared = dict(
        w_mod=_lay_kn(inp["w_mod"][0], [(b * 512, 512) for b in range(18)]), b_mod=f(inp["b_mod"]), n1=col8(inp["ffn1_norm"][0]), n2=col8(inp["mix_norm"][0]),
        n3=col8(inp["ffn2_norm"][0]), fin=f(np.asarray(inp["final_norm"]).reshape(1, 1024)),
        wg1=_lay_kn(inp["ffn1_w_gate"][0], _FG), wu1=_lay_kn(inp["ffn1_w_up"][0], _FG), wd1=_lay_rows(inp["ffn1_w_down"][0]),
        wg2=_lay_kn(inp["ffn2_w_gate"][0], _FG), wu2=_lay_kn(inp["ffn2_w_up"][0], _FG), wd2=_lay_rows(inp["ffn2_w_down"][0]),
        w_in=_lay_kn(inp["w_in"][0], [(g * 512, 512) for g in range(4)]), convw=convw_l, lamre=lamre, lamim=lamim, logdt=logdt,
        bre=bre, bim=bim, cre=cre, cim=cim, dcol=dcol, gluw=_lay_kn(inp["glu_w"][0], [(0, 512)]), glub=f(inp["glu_b"]),
        onc=col4(inp["out_norm_conv"][0]), ons=col4(inp["out_norm_ssm"][0]), wout=_lay_kn(inp["w_out"][0], [(0, 1024)]),
        identf=identf, maskc=maskc, mtab=mtab, ptab=ptab)
    in_maps = []
    for r in range(ncores):
        b, hs = r // 2, r % 2
        sel = np.zeros((128, 8), np.float32)
        if hs == 1:
            sel[:, 0] = 1.0
        m = dict(shared)
        m["x"] = f(x[b, hs * 2048:(hs + 1) * 2048])
        m["xprev"] = f(x[b, 0:2048]) if hs == 1 else np.zeros((2048, 1024), np.float32)
        m["cact_in"] = col8(cond[b])
        m["sel"] = sel
        in_maps.append(m)
    return in_maps


def kernel(**inp):
    global _NC
    in_maps = make_in_maps(inp)
    x = inp["x"]
    if _NC is None:
        _NC = build()
    res = run_bass_kernel_spmd(_NC, in_maps, core_ids=list(range(8)))
    out = np.zeros((4, 4096, 1024), np.float32)
    for r in range(8):
        b, hs = r // 2, r % 2
        out[b, hs * 2048:(hs + 1) * 2048] = np.asarray(res.results[r]["y"], dtype=np.float32)
    return out
```

```python
import math
from contextlib import ExitStack

import numpy as np
import concourse.bass as bass
import concourse.mybir as mybir
from concourse.bass_utils import run_bass_kernel_spmd

F32, BF16, I32 = mybir.dt.float32, mybir.dt.bfloat16, mybir.dt.int32
ALU = mybir.AluOpType
AF = mybir.ActivationFunctionType
ENGS = ["pe", "act", "dve", "pool", "sp"]
TWO_PI = 2.0 * math.pi
EPS = 1e-6
STRICT = False
NO_CC = False
MLIST = [1, -8, 8, 128] + [8 * m for m in range(16)] + [128 * k for k in range(16)]
NK = 32 + 2 * len(MLIST)


class Op:
    pass


class Prog:
    def __init__(self, nc, es):
        self.nc, self.es = nc, es
        self.ops = []
        self.eng_ops = {e: [] for e in ENGS}
        self.lastw, self.readers = {}, {}
        self.dma_cnt, self.sems = {}, {}
        self.last_dma = {}
        self.pending = {e: [] for e in ENGS}

    def semh(self, key):
        if key not in self.sems:
            self.sems[key] = self.es.enter_context(self.nc.semaphore("s%d" % len(self.sems)))
        return self.sems[key]

    def add(self, eng, fn, reads=(), writes=(), silent=False, dma=None):
        op = Op()
        op.eng, op.fn, op.silent = eng, fn, silent
        op.gidx, op.eidx = len(self.ops), len(self.eng_ops[eng])
        op.dma_sem, op.dma_val, op.sig = None, 0, 0
        reads = list(reads) + self.pending[eng]
        self.pending[eng] = []
        if dma is not None:
            self.last_dma[dma] = op
            op.dma_sem = dma
            self.dma_cnt[dma] = self.dma_cnt.get(dma, 0) + 1
            op.dma_val = 16 * self.dma_cnt[dma]
            self.semh(("d", dma))
        deps = []
        for k in reads:
            w = self.lastw.get(k)
            if w is not None:
                deps.append(w)
        inord = (lambda o: (not STRICT or eng == "pe") and dma is None and o.dma_sem is None and o.eng == eng
                 and eng in ("pe", "act", "dve"))
        for k in writes:
            w = self.lastw.get(k)
            if w is not None and not inord(w):
                deps.append(w)
            for r in self.readers.get(k, ()):
                if not inord(r):
                    deps.append(r)
        op.deps = deps
        for k in reads:
            self.readers.setdefault(k, []).append(op)
        for k in writes:
            self.lastw[k] = op
            self.readers[k] = []
        self.ops.append(op)
        self.eng_ops[eng].append(op)
        return op

    def barrier(self, tag):
        keys = []
        for e in ENGS:
            lastc = None
            for op in reversed(self.eng_ops[e]):
                if op.dma_sem is None:
                    lastc = op
                    break
            if lastc is not None:
                lastc.silent = False
                k = ("bar", tag, e)
                self.lastw[k] = lastc
                self.readers[k] = []
                keys.append(k)
        for sem, op in self.last_dma.items():
            k = ("bar", tag, "d", sem)
            self.lastw[k] = op
            self.readers[k] = []
            keys.append(k)
        for e in ENGS:
            self.pending[e] = self.pending[e] + keys

    def finalize(self, block):
        for op in self.ops:
            res = []
            for P in op.deps:
                if P.dma_sem is not None or not P.silent:
                    res.append(P)
                    continue
                Q = None
                for q in self.eng_ops[P.eng][P.eidx:]:
                    if q.gidx >= op.gidx:
                        break
                    if q.dma_sem is None and not q.silent:
                        Q = q
                        break
                if Q is None:
                    P.silent = False
                    Q = P
                res.append(Q)
            op.deps = res
        for e in ENGS:
            c = 0
            for op in self.eng_ops[e]:
                if op.dma_sem is None and not op.silent:
                    c += 1
                    op.sig = c
            self.semh(("e", e))

        def emit(e, eng):
            waited = {}
            for op in self.eng_ops[e]:
                need = {}
                for P in op.deps:
                    if P.dma_sem is not None:
                        s, v = ("d", P.dma_sem), P.dma_val
                    else:
                        s, v = ("e", P.eng), P.sig
                    if v > need.get(s, 0):
                        need[s] = v
                for s, v in need.items():
                    if waited.get(s, 0) < v:
                        eng.wait_ge(self.semh(s), v)
                        waited[s] = v
                ins = op.fn(eng)
                if ins is None:
                    continue
                if op.dma_sem is not None:
                    ins.then_inc(self.semh(("d", op.dma_sem)), 16)
                elif not op.silent:
                    ins.then_inc(self.semh(("e", e)), 1)

        @block.tensor
        def _(g):
            emit("pe", g)

        @block.scalar
        def _(g):
            emit("act", g)

        @block.vector
        def _(g):
            emit("dve", g)

        @block.gpsimd
        def _(g):
            emit("pool", g)

        @block.sync
        def _(g):
            emit("sp", g)


def V(t, off, dims, npart=128):
    base = t[:]
    return bass.AP(base.tensor, base.offset + off, [[base.ap[0][0], npart]] + [list(d) for d in dims])


def build(ncores=8):
    nc = bass.Bass("TRN2", target_bir_lowering=False)
    es = ExitStack()
    P = Prog(nc, es)
    global _LASTP
    _LASTP = P

    def din(name, shape, dt=F32):
        return nc.dram_tensor(name, list(shape), dt, kind="ExternalInput").ap()

    x = din("x", [2048, 1024]); xprev = din("xprev", [2048, 1024]); cact_in = din("cact_in", [128, 8])
    w_mod = din("w_mod", [128, 73728]); b_mod = din("b_mod", [1, 9216])
    nrm = [din("n%d" % i, [128, 8]) for i in (1, 2, 3)]; fin = din("fin", [1, 1024])
    wg = [din("wg%d" % i, [128, 22528]) for i in (1, 2)]
    wu = [din("wu%d" % i, [128, 22528]) for i in (1, 2)]
    wdn = [din("wd%d" % i, [128, 22528]) for i in (1, 2)]
    w_in = din("w_in", [128, 16384]); convw = din("convw", [128, 12])
    lamre = din("lamre", [128, 32]); lamim = din("lamim", [128, 32]); logdt = din("logdt", [128, 32])
    bre = din("bre", [128, 512]); bim = din("bim", [128, 512])
    cre = din("cre", [128, 512]); cim = din("cim", [128, 512]); dcol = din("dcol", [128, 32])
    gluw = din("gluw", [128, 2048]); glub = din("glub", [1, 512])
    onc = din("onc", [128, 4]); ons = din("ons", [128, 4]); wout = din("wout", [128, 8192])
    identf_d = din("identf", [128, 128]); maskc_d = din("maskc", [128, 128]); sel_d = din("sel", [128, 8])
    mtab_d = din("mtab", [128, NK]); ptab_d = din("ptab", [128, NK])
    y = nc.dram_tensor("y", [2048, 1024], F32, kind="ExternalOutput").ap()
    ssmw = nc.dram_tensor("ssmw", [4, 128, 4096], BF16, kind="Internal").ap()
    xs = nc.dram_tensor("xs", [4, 128, 8192], F32, kind="Internal").ap()
    u8s = nc.dram_tensor("u8s", [4, 128, 4096], BF16, kind="Internal").ap()
    hlas = nc.dram_tensor("hlas", [4, 128, 4096], BF16, kind="Internal").ap()
    gts = nc.dram_tensor("gts", [3, 128, 1024], F32, kind="Internal").ap()
    cc_in = nc.dram_tensor("cc_in", [128, 72], F32, kind="Internal").ap()
    cc_out = nc.dram_tensor("cc_out", [ncores * 128, 72], F32, kind="Internal").ap()

    _nm = [0]

    def sb(name, shape, dt=F32, st=None):
        _nm[0] += 1
        return (st or es).enter_context(nc.sbuf_tensor("s%d_%s" % (_nm[0], name), list(shape), dt))

    def ps(name, shape, dt=F32):
        return es.enter_context(nc.psum_tensor("p_" + name, list(shape), dt))

    ring = [sb("ring%d" % i, [128, 8, 512], BF16) for i in range(4)]
    Gc = sb("Gc", [128, 1024])
    identf = sb("identf", [128, 128]); identb = sb("identb", [128, 128], BF16)
    onesf = sb("onesf", [128, 128]); onesb = sb("onesb", [1, 128], BF16)
    Gm = sb("Gm", [128, 24]); modT = sb("modT", [128, 72])
    convw_s = sb("convw_s", [128, 12]); onc_s = sb("onc_s", [128, 4]); ons_s = sb("ons_s", [128, 4])
    sel_s = sb("sel_s", [128, 8]); glub_b = sb("glub_b", [1, 512], BF16)
    EP = sb("EP", [128, 32, 2 * len(MLIST)])
    HEA = sb("HEA", [128, 32, 32]); HEB = sb("HEB", [128, 32, 32])
    HIA = sb("HIA", [128, 32, 33]); HIB = sb("HIB", [128, 32, 33])
    PAY = sb("PAY", [128, 72]); HIN = sb("HIN", [128, 72])
    zlast = sb("zlast", [128, 4, 8], BF16)
    sst = sb("sst", [128, 64])
    pb = [ps("pb%d" % i, [128, 512]) for i in range(7)]
    pt = ps("pt", [128, 1024], BF16)
    NL = len(MLIST)

    cnt = {"ring": 0, "xn": 0, "ss": 0, "sg": 0, "ta": 0, "gu": 0, "dn": 0}

    def rot(name, n):
        v = cnt[name] % n
        cnt[name] += 1
        return v

    def dve(fn, reads, writes):
        return P.add("dve", fn, reads, writes)

    def act(fn, reads, writes):
        return P.add("act", fn, reads, writes)

    def load(eng, out_ap, in_ap, key, sem):
        return P.add(eng, lambda g: g.dma_start(out=out_ap, in_=in_ap), writes=[key], dma=sem)

    xr = x.rearrange("(h s j) d -> s h j d", h=2, s=128, j=8)
    xpr = xprev.rearrange("(h s j) d -> s h j d", h=2, s=128, j=8)

    st = ExitStack()
    cact = sb("cact", [128, 8], F32, st); cactb = sb("cactb", [128, 8], BF16, st)
    modrow = [sb("modrow%d" % i, [1, 512], F32, st) for i in range(2)]
    bmod_s = [sb("bmods%d" % i, [1, 512], F32, st) for i in range(2)]
    Gs = sb("Gs", [128, 1024], F32, st)
    nrm_s = sb("nrm_s", [128, 24], F32, st)
    one11 = sb("one11", [1, 1], F32, st)
    lre = sb("lre", [128, 32], F32, st); lim = sb("lim", [128, 32], F32, st); ldt = sb("ldt", [128, 32], F32, st)
    Br = sb("Br", [128, 32, 16], F32, st); Bi = sb("Bi", [128, 32, 16], F32, st)
    Cr = sb("Cr", [128, 32, 16], F32, st); Ci = sb("Ci", [128, 32, 16], F32, st)
    dcol_s = sb("dcol_s", [128, 32], F32, st); maskc = sb("maskc", [128, 128], F32, st)
    mtab = sb("mtab", [128, NK], F32, st); ptab = sb("ptab", [128, NK], F32, st)
    glub_f = sb("glub_f", [1, 512], F32, st)
    TAB = sb("TAB", [128, 32, NK], F32, st); T1 = sb("T1", [128, 32, NK], F32, st)
    T2 = sb("T2", [128, 32, NK], F32, st); TI = sb("TI", [128, 32, NK], I32, st)
    adt = sb("adt", [128, 32], F32, st); bdt = sb("bdt", [128, 32], F32, st)
    sm = [sb("sm%d" % i, [128, 32], F32, st) for i in range(8)]
    WA = sb("WA", [128, 32, 8], F32, st); WB = sb("WB", [128, 32, 8], F32, st)
    WA2 = sb("WA2", [128, 32, 8], F32, st); WB2 = sb("WB2", [128, 32, 8], F32, st)
    W8 = [sb("W8%d" % i, [128, 32, 8], F32, st) for i in range(2)]
    LBSA = sb("LBSA", [128, 8, 128], F32, st); LBSB = sb("LBSB", [128, 8, 128], F32, st)
    QS = sb("QS", [128, 8, 128], F32, st); RS = sb("RS", [128, 8, 128], F32, st)
    OT = [sb("OT%d" % i, [128, 8, 128], F32, st) for i in range(2)]
    STG = [sb("STG%d" % i, [128, 8, 128], BF16, st) for i in range(4)]
    KTt = sb("KTt", [128, 128], F32, st)

    smalls = [(cact, cact_in, "cact"), (lre, lamre, "lre"), (lim, lamim, "lim"),
              (ldt, logdt, "ldt"), (dcol_s, dcol, "dcol"), (maskc, maskc_d, "maskc"), (mtab, mtab_d, "mtab"),
              (ptab, ptab_d, "ptab"), (glub_f, glub, "glubf"), (identf, identf_d, "identf"),
              (convw_s, convw, "convw"), (onc_s, onc, "onc"), (ons_s, ons, "ons"), (sel_s, sel_d, "sel")]
    last = None
    keys = []
    for t, src, k in smalls:
        last = P.add("sp", lambda g, t=t, src=src: g.dma_start(out=t[:], in_=src), writes=[k], dma="small")
        keys.append(k)
    for i in range(3):
        last = P.add("sp", lambda g, i=i: g.dma_start(out=nrm_s[:, i * 8:(i + 1) * 8], in_=nrm[i]),
                     writes=["nrm%d" % i], dma="small")
        keys.append("nrm%d" % i)
    for t, src, k in ((Br, bre, "Br"), (Bi, bim, "Bi"), (Cr, cre, "Cr"), (Ci, cim, "Ci")):
        last = P.add("sp", lambda g, t=t, src=src: g.dma_start(out=t[:].rearrange("p a b -> p (a b)"), in_=src),
                     writes=[k], dma="small")
        keys.append(k)
    for k in keys:
        P.lastw[k] = last

    dve(lambda g: g.memset(onesf[:], 1.0), [], ["onesf"])
    dve(lambda g: g.memset(onesb[:], 1.0), [], ["onesb"])
    dve(lambda g: g.memset(one11[:], 1.0), [], ["one11"])
    dve(lambda g: g.tensor_copy(out=identb[:], in_=identf[:]), ["identf"], ["identb"])
    dve(lambda g: g.tensor_copy(out=glub_b[:], in_=glub_f[:]), ["glubf"], ["glubb"])
    act(lambda g: g.activation(out=cact[:], in_=cact[:], func=AF.Silu), ["cact"], ["cact"])
    dve(lambda g: g.tensor_copy(out=cactb[:], in_=cact[:]), ["cact"], ["cactb"])

    def cast_load(dst_ap, src_ap, dst_key, a, b):
        P.add("pool", lambda g: g.dma_start(out=dst_ap, in_=src_ap), writes=[dst_key], dma="cl_" + str(dst_key))

    def ring_load(src_ap, key):
        s = rot("ring", 4)
        cast_load(ring[s][:], src_ap, ("ring", s), 8, 512)
        return s

    gate_blk = {4: (0, 0, 0.5), 5: (0, 1, 0.5), 10: (1, 0, 1.0), 11: (1, 1, 1.0), 16: (2, 0, 0.5), 17: (2, 1, 0.5)}
    for blk in range(18):
        s = ring_load(w_mod[:, blk * 4096:(blk + 1) * 4096].rearrange("p (k n) -> p k n", k=8), None)
        bank = pb[blk % 2]
        mi = blk % 2
        P.add("sp", lambda g, blk=blk, mi=mi: g.dma_start(out=bmod_s[mi][:], in_=b_mod[0:1, blk * 512:(blk + 1) * 512]),
              writes=[("bmod", mi)], dma="bmod%d" % mi)
        for kc in range(8):
            P.add("pe", lambda g, kc=kc, s=s, bank=bank: g.matmul(bank[0:1, :], lhsT=cactb[:, kc:kc + 1],
                                                                    rhs=ring[s][:, kc, :], start=(kc == 0), stop=(kc == 7)),
                  reads=[("ring", s), "cactb"], writes=[("pb", blk % 2)], silent=(kc < 7))
        dve(lambda g, mi=mi, bank=bank: g.tensor_tensor(out=modrow[mi][:], in0=bank[0:1, :], in1=bmod_s[mi][:], op=ALU.add),
            [("pb", blk % 2), ("bmod", mi)], [("modrow", mi)])
        for q4 in range(4):
            q = blk * 4 + q4
            P.add("pe", lambda g, q=q, q4=q4, mi=mi: g.matmul(pb[2][:, q:q + 1], lhsT=modrow[mi][0:1, q4 * 128:(q4 + 1) * 128],
                                                              rhs=one11[0:1, 0:1], start=True, stop=True),
                  reads=[("modrow", mi), "one11"], writes=[("pb2c", q)], silent=(q4 < 3))
        if blk in gate_blk:
            gi_, nh, sc = gate_blk[blk]
            P.add("pe", lambda g, mi=mi: g.matmul(pb[3][:, :], lhsT=onesf[0:1, :], rhs=modrow[mi][:], start=True, stop=True),
                  reads=[("modrow", mi), "onesf"], writes=[("pb", 3)])
            act(lambda g, nh=nh, sc=sc: g.mul(Gs[:, nh * 512:(nh + 1) * 512], pb[3][:, :], sc), [("pb", 3)], ["Gs"])
            if nh == 1:
                P.add("sp", lambda g, gi_=gi_: g.dma_start(out=gts[gi_], in_=Gs[:]), reads=["Gs"], writes=[("gts", gi_)],
                      dma="gts")
    dve(lambda g: g.tensor_copy(out=modT[:], in_=pb[2][:, 0:72]), [("pb2c", q) for q in range(72)], ["modT"])
    for n in range(3):
        dve(lambda g, n=n: g.scalar_tensor_tensor(out=Gm[:, n * 8:(n + 1) * 8], in0=modT[:, 24 * n + 8:24 * n + 16],
                                                  scalar=1.0, in1=nrm_s[:, n * 8:(n + 1) * 8], op0=ALU.add, op1=ALU.mult),
            ["modT", "nrm%d" % n], ["Gm"])

    act(lambda g: g.activation(out=ldt[:], in_=ldt[:], func=AF.Exp), ["ldt"], ["dt"])
    dve(lambda g: g.tensor_tensor(out=adt[:], in0=lre[:], in1=ldt[:], op=ALU.mult), ["lre", "dt"], ["adt"])
    dve(lambda g: g.tensor_tensor(out=bdt[:], in0=lim[:], in1=ldt[:], op=ALU.mult), ["lim", "dt"], ["bdt"])
    bG = lambda t: V(t, 0, [[1, 32], [0, NK]])
    bK = lambda t: V(t, 0, [[0, 32], [1, NK]])
    dve(lambda g: g.tensor_tensor(out=T1[:], in0=bG(bdt), in1=bK(mtab), op=ALU.mult), ["bdt", "mtab"], ["T1"])
    dve(lambda g: g.tensor_tensor(out=T1[:], in0=T1[:], in1=bK(ptab), op=ALU.add), ["T1", "ptab"], ["T1"])
    dve(lambda g: g.tensor_scalar(out=T2[:], in0=T1[:], scalar1=1.0 / TWO_PI, scalar2=None, op0=ALU.mult), ["T1"], ["T2"])
    dve(lambda g: g.tensor_copy(out=TI[:], in_=T2[:]), ["T2"], ["TI"])
    dve(lambda g: g.tensor_copy(out=T2[:], in_=TI[:]), ["TI"], ["T2"])
    dve(lambda g: g.scalar_tensor_tensor(out=T1[:], in0=T2[:], scalar=-TWO_PI, in1=T1[:], op0=ALU.mult, op1=ALU.add),
        ["T2", "T1"], ["T1"])
    dve(lambda g: g.tensor_scalar(out=T2[:], in0=T1[:], scalar1=math.pi, scalar2=None, op0=ALU.is_gt), ["T1"], ["T2"])
    dve(lambda g: g.scalar_tensor_tensor(out=T1[:], in0=T2[:], scalar=-TWO_PI, in1=T1[:], op0=ALU.mult, op1=ALU.add),
        ["T2", "T1"], ["T1"])
    dve(lambda g: g.tensor_scalar(out=T2[:], in0=T1[:], scalar1=-math.pi, scalar2=None, op0=ALU.is_lt), ["T1"], ["T2"])
    dve(lambda g: g.scalar_tensor_tensor(out=T1[:], in0=T2[:], scalar=TWO_PI, in1=T1[:], op0=ALU.mult, op1=ALU.add),
        ["T2", "T1"], ["T1"])
    dve(lambda g: g.tensor_scalar(out=T1[:], in0=T1[:], scalar1=math.pi, scalar2=-math.pi, op0=ALU.min, op1=ALU.max),
        ["T1"], ["T1"])
    act(lambda g: g.activation(out=T1[:], in_=T1[:], func=AF.Sin), ["T1"], ["T1"])
    dve(lambda g: g.tensor_tensor(out=T2[:], in0=bG(adt), in1=bK(mtab), op=ALU.mult), ["adt", "mtab"], ["T2"])
    act(lambda g: g.activation(out=T2[:], in_=T2[:], func=AF.Exp), ["T2"], ["T2"])
    dve(lambda g: g.tensor_tensor(out=TAB[:], in0=T1[:], in1=T2[:], op=ALU.mult), ["T1", "T2"], ["TAB"])
    dve(lambda g: g.tensor_copy(out=EP[:], in_=TAB[:, :, 32:NK]), ["TAB"], ["EP"])
    e1r, e1i = TAB[:, :, 32], TAB[:, :, 32 + NL]
    nr, den, cr, ci, t0, t1_ = sm[0], sm[1], sm[2], sm[3], sm[4], sm[5]
    dve(lambda g: g.tensor_scalar(out=nr[:], in0=e1r, scalar1=-1.0, scalar2=None, op0=ALU.add), ["TAB"], ["nr"])
    dve(lambda g: g.tensor_tensor(out=den[:], in0=lre[:], in1=lre[:], op=ALU.mult), ["lre"], ["den"])
    dve(lambda g: g.tensor_tensor(out=t0[:], in0=lim[:], in1=lim[:], op=ALU.mult), ["lim"], ["t0"])
    dve(lambda g: g.tensor_tensor(out=den[:], in0=den[:], in1=t0[:], op=ALU.add), ["den", "t0"], ["den"])
    dve(lambda g: g.reciprocal(out=den[:], in_=den[:]), ["den"], ["den"])
    dve(lambda g: g.tensor_tensor(out=cr[:], in0=nr[:], in1=lre[:], op=ALU.mult), ["nr", "lre"], ["cr"])
    dve(lambda g: g.tensor_tensor(out=t0[:], in0=e1i, in1=lim[:], op=ALU.mult), ["TAB", "lim"], ["t0"])
    dve(lambda g: g.tensor_tensor(out=cr[:], in0=cr[:], in1=t0[:], op=ALU.add), ["cr", "t0"], ["cr"])
    dve(lambda g: g.tensor_tensor(out=cr[:], in0=cr[:], in1=den[:], op=ALU.mult), ["cr", "den"], ["cr"])
    dve(lambda g: g.tensor_tensor(out=ci[:], in0=e1i, in1=lre[:], op=ALU.mult), ["TAB", "lre"], ["ci"])
    dve(lambda g: g.tensor_tensor(out=t0[:], in0=nr[:], in1=lim[:], op=ALU.mult), ["nr", "lim"], ["t0"])
    dve(lambda g: g.tensor_tensor(out=ci[:], in0=ci[:], in1=t0[:], op=ALU.subtract), ["ci", "t0"], ["ci"])
    dve(lambda g: g.tensor_tensor(out=ci[:], in0=ci[:], in1=den[:], op=ALU.mult), ["ci", "den"], ["ci"])
    b8 = lambda t: V(t, 0, [[1, 32], [0, 8]])
    SAE, SBE = TAB[:, :, 0:8], TAB[:, :, 8:16]

    def cplx(outA, outB, xa, xb, yr, yi, ka, kb):
        dve(lambda g: g.tensor_tensor(out=W8[0][:], in0=xa, in1=yr, op=ALU.mult), ka, ["W80"])
        dve(lambda g: g.tensor_tensor(out=W8[1][:], in0=xb, in1=yi, op=ALU.mult), ka, ["W81"])
        dve(lambda g: g.tensor_tensor(out=outA[:], in0=W8[0][:], in1=W8[1][:], op=ALU.add), ["W80", "W81"], [kb + "A"])
        dve(lambda g: g.tensor_tensor(out=W8[0][:], in0=xb, in1=yr, op=ALU.mult), ka, ["W80"])
        dve(lambda g: g.tensor_tensor(out=W8[1][:], in0=xa, in1=yi, op=ALU.mult), ka, ["W81"])
        dve(lambda g: g.tensor_tensor(out=outB[:], in0=W8[0][:], in1=W8[1][:], op=ALU.subtract), ["W80", "W81"], [kb + "B"])

    cplx(WA, WB, SAE, SBE, b8(cr), b8(ci), ["TAB", "cr", "ci"], "W")
    e8r = V(TAB, 32 + 1, [[NK, 32], [0, 8]])
    e8i = V(TAB, 32 + NL + 1, [[NK, 32], [0, 8]])
    cplx(WA2, WB2, WA[:], WB[:], e8r, e8i, ["TAB", "WA", "WB"], "W2")
    TC1, TC2 = 16, 24

    def outer(dst, w1, o1, m1, w2, o2, m2, sub, gb, kw, kd):
        def wv(w, o):
            if w is TAB:
                return V(TAB, gb * 8 * NK + o, [[NK, 8], [1, 8], [0, 16]])
            return V(w, gb * 8 * 8, [[8, 8], [1, 8], [0, 16]])
        mv = lambda m: V(m, gb * 8 * 16, [[16, 8], [0, 8], [1, 16]])
        d4 = lambda t: V(t, 0, [[128, 8], [16, 8], [1, 16]])
        dve(lambda g: g.tensor_tensor(out=d4(OT[0]), in0=wv(w1, o1), in1=mv(m1), op=ALU.mult), kw, ["OT0"])
        dve(lambda g: g.tensor_tensor(out=d4(OT[1]), in0=wv(w2, o2), in1=mv(m2), op=ALU.mult), kw, ["OT1"])
        dve(lambda g: g.tensor_tensor(out=dst[:], in0=OT[0][:], in1=OT[1][:], op=(ALU.subtract if sub else ALU.add)),
            ["OT0", "OT1"], [kd])

    kw_all = ["TAB", "WAA", "WAB", "W2A", "W2B", "Br", "Bi", "Cr", "Ci"]
    for gb in range(4):
        outer(LBSA, WA, 0, Br, WB, 0, Bi, False, gb, kw_all, "LBSA")
        outer(LBSB, WB, 0, Br, WA, 0, Bi, True, gb, kw_all, "LBSB")
        outer(QS, WA2, 0, Br, WB2, 0, Bi, False, gb, kw_all, "QS")
        outer(RS, TAB, TC1, Cr, TAB, TC2, Ci, False, gb, kw_all, "RS")
        for src, kk, si in ((LBSA, "LBSA", 0), (LBSB, "LBSB", 1)):
            for half in range(2):
                bk = 5 + half
                for g4 in range(4):
                    gl = half * 4 + g4
                    P.add("pe", lambda g, src=src, gl=gl, g4=g4, bk=bk: g.transpose(pb[bk][:, g4 * 128:(g4 + 1) * 128],
                                                                                   src[:, gl, :], identf[:]),
                          reads=[kk, "identf"], writes=[("pb", bk)], silent=(g4 < 3))
                dve(lambda g, si=si, half=half, bk=bk: g.tensor_copy(
                    out=STG[si][:, half * 4:(half + 1) * 4, :], in_=pb[bk][:, :].rearrange("p (a b) -> p a b", a=4)),
                    [("pb", bk)], [("STG", si)])
        for gl in range(8):
            gg = gb * 8 + gl
            P.add("pe", lambda g, gl=gl: g.matmul(pb[4][:, 0:128], lhsT=QS[:, gl, :], rhs=RS[:, gl, :], start=True, stop=True),
                  reads=["QS", "RS"], writes=[("pb", 4)])
            dve(lambda g: g.tensor_tensor(out=KTt[:], in0=pb[4][:, 0:128], in1=maskc[:], op=ALU.mult),
                [("pb", 4), "maskc"], ["KTt"])
            dve(lambda g, gl=gl, gg=gg: g.scalar_tensor_tensor(out=STG[2][:, gl, :], in0=identf[:], scalar=dcol_s[:, gg:gg + 1],
                                                              in1=KTt[:], op0=ALU.mult, op1=ALU.add),
                ["KTt", "identf", "dcol"], [("STG", 2)])
        act(lambda g: g.copy(out=STG[3][:], in_=RS[:]), ["RS"], [("STG", 3)])
        for si in range(4):
            P.add("sp", lambda g, si=si, gb=gb: g.dma_start(out=ssmw[si, :, gb * 1024:(gb + 1) * 1024],
                                                            in_=STG[si][:].rearrange("p a b -> p (a b)")),
                  reads=[("STG", si)], writes=[("ssmw", si)], dma="ssmw%d" % si)
    P.barrier("setup")
    st.close()
    X = sb("X", [128, 8, 1024])
    hT = sb("hT", [128, 8, 1024], BF16)
    U8 = sb("U8", [128, 32, 128], BF16)
    HLA = sb("HLA", [128, 32, 128], BF16)
    xn = [sb("xn%d" % i, [128, 1024], BF16) for i in range(2)]
    junk = sb("junk", [128, 1024], BF16)
    sgt = [sb("sgt%d" % i, [128, 512]) for i in range(2)]
    tmpA = [sb("tmpA%d" % i, [128, 512]) for i in range(2)]

    Xk = [("X", j) for j in range(8)]

    def rstd_from(ss_ap, n_el, key_in, key_out):
        dve(lambda g: g.tensor_scalar(out=ss_ap, in0=ss_ap, scalar1=1.0 / n_el, scalar2=EPS, op0=ALU.mult, op1=ALU.add),
            [key_in], [key_out])
        act(lambda g: g.activation(out=ss_ap, in_=ss_ap, func=AF.Sqrt), [key_out], [key_out])
        dve(lambda g: g.reciprocal(out=ss_ap, in_=ss_ap), [key_out], [key_out])

    def norm_to_hT(n):
        for j in range(8):
            c = rot("ss", 64)
            ssa = sst[:, c:c + 1]
            xb = rot("xn", 2)
            act(lambda g, j=j, ssa=ssa: g.activation(out=junk[:], in_=X[:, j, :], func=AF.Square, accum_out=ssa),
                [("X", j)], ["junk", ("ss", c)])
            rstd_from(ssa, 1024.0, ("ss", c), ("ss", c))
            dve(lambda g, j=j, ssa=ssa, xb=xb: g.tensor_scalar(out=xn[xb][:], in0=X[:, j, :], scalar1=ssa,
                                                               scalar2=None, op0=ALU.mult),
                [("X", j), ("ss", c)], [("xn", xb)])
            for kc in range(8):
                P.add("pe", lambda g, kc=kc, xb=xb: g.transpose(pt[:, kc * 128:(kc + 1) * 128],
                                                                 xn[xb][:, kc * 128:(kc + 1) * 128], identb[:]),
                      reads=[("xn", xb), "identb"], writes=["pt"], silent=(kc < 7))
            for kc in range(8):
                act(lambda g, kc=kc, j=j, n=n: g.activation(out=hT[:, kc, j * 128:(j + 1) * 128],
                                                             in_=pt[:, kc * 128:(kc + 1) * 128], func=AF.Identity,
                                                             scale=Gm[:, n * 8 + kc:n * 8 + kc + 1],
                                                             bias=modT[:, 24 * n + kc:24 * n + kc + 1]),
                    ["pt", "Gm", "modT"], [("hT", j)])

    hT_all = [("hT", j) for j in range(8)]

    def load_G(i):
        P.add("sp", lambda g: g.dma_start(out=Gc[:], in_=gts[i]), reads=[("gts", i)], writes=["Gc"], dma="gc")

    def ffn(fi, actT, wds):
        wgr, wur, wdr = wg[fi], wu[fi], wdn[fi]
        k8 = lambda ap: ap.rearrange("p (k n) -> p k n", k=8)
        for (f0, nf) in ((0, 8), (8, 8), (16, 6)):
            for w0_ in range(0, nf, 4):
                wn_ = min(4, nf - w0_)
                cast_load(wds[:, w0_:w0_ + wn_, :], wdr[:, (f0 + w0_) * 1024:(f0 + w0_ + wn_) * 1024].rearrange("p (a b) -> p a b", a=wn_), "wds", wn_, 1024)
            for q0 in range(0, nf, 4):
                nq = min(4, nf - q0)
                c0 = (f0 + q0) * 128
                sg_ = rot("ring", 4)
                cast_load(ring[sg_][:, :, 0:nq * 128], k8(wgr[:, 8 * c0:8 * (c0 + nq * 128)]), ("ring", sg_), 8, nq * 128)
                su_ = rot("ring", 4)
                cast_load(ring[su_][:, :, 0:nq * 128], k8(wur[:, 8 * c0:8 * (c0 + nq * 128)]), ("ring", su_), 8, nq * 128)
                for fq in range(nq):
                    fl = q0 + fq
                    for nb in range(2):
                        pr = rot("gu", 2)
                        bg, bu = 2 * pr, 2 * pr + 1
                        for kc in range(8):
                            P.add("pe", lambda g, kc=kc, fq=fq, nb=nb, bg=bg, sg_=sg_: g.matmul(
                                pb[bg][:, :], lhsT=ring[sg_][:, kc, fq * 128:(fq + 1) * 128],
                                rhs=hT[:, kc, nb * 512:(nb + 1) * 512], start=(kc == 0), stop=(kc == 7)),
                                reads=[("ring", sg_)] + hT_all, writes=[("pb", bg)], silent=(kc < 7))
                        for kc in range(8):
                            P.add("pe", lambda g, kc=kc, fq=fq, nb=nb, bu=bu, su_=su_: g.matmul(
                                pb[bu][:, :], lhsT=ring[su_][:, kc, fq * 128:(fq + 1) * 128],
                                rhs=hT[:, kc, nb * 512:(nb + 1) * 512], start=(kc == 0), stop=(kc == 7)),
                                reads=[("ring", su_)] + hT_all, writes=[("pb", bu)], silent=(kc < 7))
                        si = rot("sg", 2)
                        act(lambda g, si=si, bg=bg: g.activation(out=sgt[si][:], in_=pb[bg][:, :], func=AF.Silu),
                            [("pb", bg)], [("sgt", si)])
                        dve(lambda g, si=si, bu=bu, fl=fl, nb=nb: g.tensor_tensor(
                            out=actT[:, fl, nb * 512:(nb + 1) * 512], in0=sgt[si][:], in1=pb[bu][:, :], op=ALU.mult),
                            [("sgt", si), ("pb", bu)], [("actT", fl)])
            for j in range(8):
                for nh in range(2):
                    bk = 4 + rot("dn", 2)
                    for fl in range(nf):
                        P.add("pe", lambda g, fl=fl, j=j, nh=nh, bk=bk, nf=nf: g.matmul(
                            pb[bk][:, :], lhsT=actT[:, fl, j * 128:(j + 1) * 128],
                            rhs=wds[:, fl, nh * 512:(nh + 1) * 512], start=(fl == 0), stop=(fl == nf - 1)),
                            reads=[("actT", fl), "wds"], writes=[("pb", bk)], silent=(fl < nf - 1))
                    ti = rot("ta", 2)
                    dve(lambda g, ti=ti, bk=bk, nh=nh: g.tensor_tensor(out=tmpA[ti][:], in0=pb[bk][:, :],
                                                                      in1=Gc[:, nh * 512:(nh + 1) * 512], op=ALU.mult),
                        [("pb", bk), "Gc"], [("tmpA", ti)])
                    dve(lambda g, ti=ti, j=j, nh=nh: g.tensor_tensor(
                        out=X[:, j, nh * 512:(nh + 1) * 512], in0=X[:, j, nh * 512:(nh + 1) * 512],
                        in1=tmpA[ti][:], op=ALU.add),
                        [("tmpA", ti), ("X", j)], [("X", j)])

    wig = lambda gi: w_in[:, gi * 4096:(gi + 1) * 4096].rearrange("p (k n) -> p k n", k=8)
    flat3 = lambda t: t[:].rearrange("p a b -> p (a b)")

    load_G(0)

    def phase1(h):
        s1 = ExitStack()
        actT = sb("actT", [128, 8, 1024], BF16, s1)
        wds = sb("wds", [128, 8, 1024], BF16, s1)
        P.add("sp", lambda g, h=h: g.dma_start(out=X[:], in_=(xpr[:, h] if h < 2 else xr[:, h - 2])), writes=Xk, dma="x")
        norm_to_hT(0)
        ffn(0, actT, wds)
        P.add("sp", lambda g, h=h: g.dma_start(out=xs[h], in_=X[:].rearrange("p a b -> p (a b)")),
              reads=Xk, writes=[("xs", h)], dma="xs")
        norm_to_hT(1)
        P.barrier("p1a%d" % h)
        s1.close()
        s1 = ExitStack()
        Utok = sb("Utok", [128, 32, 8, 16], BF16, s1)
        swA = sb("swA", [128, 32, 128], BF16, s1); swB = sb("swB", [128, 32, 128], BF16, s1)
        SAb = sb("SAb", [128, 32, 128], F32, s1); SBb = sb("SBb", [128, 32, 128], F32, s1)
        sc = [sb("sc%d" % i, [128, 32, 8], F32, s1) for i in range(4)]
        ctmp = sb("ctmp", [128, 2], F32, s1)
        su = ring_load(wig(3), None)
        for j in range(8):
            for kc in range(8):
                P.add("pe", lambda g, kc=kc, j=j: g.matmul(pb[0][:, :], lhsT=hT[:, kc, j * 128:(j + 1) * 128],
                                                           rhs=ring[su][:, kc, :], start=(kc == 0), stop=(kc == 7)),
                      reads=[("ring", su), ("hT", j)], writes=[("pb", 0)], silent=(kc < 7))
            act(lambda g, j=j: g.copy(out=Utok[:, :, j, :], in_=pb[0][:, :].rearrange("p (a b) -> p a b", a=32)),
                [("pb", 0)], ["Utok"])
        for gb in range(4):
            for g8 in range(8):
                gg = gb * 8 + g8
                P.add("pe", lambda g, gg=gg, g8=g8: g.transpose(pt[:, g8 * 128:(g8 + 1) * 128],
                                                               Utok[:, gg].rearrange("p a b -> p (a b)"), identb[:]),
                      reads=["Utok", "identb"], writes=["pt"], silent=(g8 < 7))
            dve(lambda g, gb=gb: g.tensor_copy(out=U8[:, gb * 8:(gb + 1) * 8, :],
                                               in_=pt[:, :].rearrange("p (a b) -> p a b", a=8)),
                ["pt"], ["U8"])
        P.add("sp", lambda g, h=h: g.dma_start(out=u8s[h], in_=flat3(U8)), reads=["U8"], writes=[("u8s", h)], dma="u8s")
        if h == 1:
            scv = ring_load(wig(1), None)
            svv = ring_load(wig(2), None)
            halo_rhs = lambda kc: V(hT, kc * 1024 + 6 * 128 + 127, [[128, 2]])
            for cc in range(4):
                for kc in range(8):
                    P.add("pe", lambda g, kc=kc, cc=cc: g.matmul(pb[1][:, 0:2], lhsT=ring[scv][:, kc, cc * 128:(cc + 1) * 128],
                                                                 rhs=halo_rhs(kc), start=(kc == 0), stop=(kc == 7)),
                          reads=[("ring", scv)] + hT_all, writes=[("pb", 1)], silent=(kc < 7))
                for kc in range(8):
                    P.add("pe", lambda g, kc=kc, cc=cc: g.matmul(pb[2][:, 0:2], lhsT=ring[svv][:, kc, cc * 128:(cc + 1) * 128],
                                                                 rhs=halo_rhs(kc), start=(kc == 0), stop=(kc == 7)),
                          reads=[("ring", svv)] + hT_all, writes=[("pb", 2)], silent=(kc < 7))
                act(lambda g: g.copy(out=ctmp[:], in_=pb[1][:, 0:2]), [("pb", 1)], ["ctmp"])
                dve(lambda g, cc=cc: g.tensor_tensor(out=PAY[:, 64 + cc * 2:66 + cc * 2], in0=ctmp[:], in1=pb[2][:, 0:2],
                                                     op=ALU.mult), ["ctmp", ("pb", 2)], ["PAY"])
        for gh in range(1):
            for (swt, si) in ((swA, 0), (swB, 1)):
                P.add("sp", lambda g, swt=swt, si=si, gh=gh: g.dma_start(out=flat3(swt), in_=ssmw[si]),
                      reads=[("ssmw", si)], writes=[("sw", si)], dma="sw%d" % si)
            for (dstb, swt, si, kk) in ((SAb, swA, 0, "SAb"), (SBb, swB, 1, "SBb")):
                for gq in range(8):
                    bk = 3 + (gq % 2)
                    for g4 in range(4):
                        gl = gq * 4 + g4
                        P.add("pe", lambda g, gl=gl, g4=g4, bk=bk, swt=swt, gh=gh: g.matmul(
                            pb[bk][:, g4 * 128:(g4 + 1) * 128], lhsT=swt[:, gl, :], rhs=U8[:, gl, :],
                            start=True, stop=True),
                            reads=[("sw", si), "U8"], writes=[("pb", bk)], silent=(g4 < 3))
                    act(lambda g, dstb=dstb, gq=gq, bk=bk: g.copy(out=dstb[:, gq * 4:(gq + 1) * 4, :],
                                                                 in_=pb[bk][:, :].rearrange("p (a b) -> p a b", a=4)),
                        [("pb", bk)], [kk])
            vm = lambda t, m: V(t, m, [[128, 32], [16, 8]])
            hv = lambda m, gh=gh: V(HLA, m, [[128, 32], [16, 8]])
            Er = V(EP, 2, [[2 * NL, 32], [0, 8]])
            Ei = V(EP, NL + 2, [[2 * NL, 32], [0, 8]])
            dve(lambda g, hv=hv: g.memset(hv(0), 0.0), [], ["HLA"])
            for m in range(1, 16):
                a_c, b_c = vm(SAb, m - 1), vm(SBb, m - 1)
                dve(lambda g, m=m, a_c=a_c, hv=hv: g.tensor_copy(out=hv(m), in_=a_c), ["SAb"], ["HLA"])
                dve(lambda g, a_c=a_c, Er=Er: g.tensor_tensor(out=sc[0][:], in0=a_c, in1=Er, op=ALU.mult), ["SAb", "EP"], ["sc0"])
                dve(lambda g, b_c=b_c, Ei=Ei: g.tensor_tensor(out=sc[1][:], in0=b_c, in1=Ei, op=ALU.mult), ["SBb", "EP"], ["sc1"])
                dve(lambda g: g.tensor_tensor(out=sc[0][:], in0=sc[0][:], in1=sc[1][:], op=ALU.add), ["sc0", "sc1"], ["sc0"])
                dve(lambda g, b_c=b_c, Er=Er: g.tensor_tensor(out=sc[2][:], in0=b_c, in1=Er, op=ALU.mult), ["SBb", "EP"], ["sc2"])
                dve(lambda g, a_c=a_c, Ei=Ei: g.tensor_tensor(out=sc[3][:], in0=a_c, in1=Ei, op=ALU.mult), ["SAb", "EP"], ["sc3"])
                dve(lambda g: g.tensor_tensor(out=sc[2][:], in0=sc[2][:], in1=sc[3][:], op=ALU.subtract), ["sc2", "sc3"], ["sc2"])
                dve(lambda g, m=m: g.tensor_tensor(out=vm(SAb, m), in0=vm(SAb, m), in1=sc[0][:], op=ALU.add),
                    ["SAb", "sc0"], ["SAb"])
                dve(lambda g, m=m: g.tensor_tensor(out=vm(SBb, m), in0=vm(SBb, m), in1=sc[2][:], op=ALU.add),
                    ["SBb", "sc2"], ["SBb"])
            dve(lambda g, gh=gh, h=h: g.tensor_copy(out=HEA[:, :, h * 8:(h + 1) * 8], in_=vm(SAb, 15)),
                ["SAb"], ["HEA"])
            dve(lambda g, gh=gh, h=h: g.tensor_copy(out=HEB[:, :, h * 8:(h + 1) * 8], in_=vm(SBb, 15)),
                ["SBb"], ["HEB"])
        P.add("sp", lambda g, h=h: g.dma_start(out=hlas[h], in_=flat3(HLA)), reads=["HLA"], writes=[("hlas", h)], dma="hlas")
        P.barrier("p1b%d" % h)
        s1.close()

    phase1(0)
    phase1(1)
    phase1(2)
    phase1(3)

    s2 = ExitStack()
    cs = [sb("cs%d" % i, [128, 32], F32, s2) for i in range(4)]
    E128r, E128i = EP[:, :, 3], EP[:, :, NL + 3]
    flag = sel_s[:, 0:1]
    dve(lambda g: g.tensor_scalar(out=HEA[:, :, 0:16], in0=HEA[:, :, 0:16], scalar1=flag, scalar2=None, op0=ALU.mult),
        ["HEA", "sel"], ["HEA"])
    dve(lambda g: g.tensor_scalar(out=HEB[:, :, 0:16], in0=HEB[:, :, 0:16], scalar1=flag, scalar2=None, op0=ALU.mult),
        ["HEB", "sel"], ["HEB"])
    dve(lambda g: g.tensor_scalar(out=HIN[:, 64:72], in0=PAY[:, 64:72], scalar1=flag, scalar2=None, op0=ALU.mult), ["PAY", "sel"], ["HIN"])
    dve(lambda g: g.memset(HIA[:, :, 0], 0.0), [], ["HIA"])
    dve(lambda g: g.memset(HIB[:, :, 0], 0.0), [], ["HIB"])
    for ck in range(32):
        oa, ob = HIA[:, :, ck + 1], HIB[:, :, ck + 1]
        ia, ib = HIA[:, :, ck], HIB[:, :, ck]
        dve(lambda g, ia=ia: g.tensor_tensor(out=cs[0][:], in0=ia, in1=E128r, op=ALU.mult), ["HIA", "EP"], ["cs0"])
        dve(lambda g, ib=ib: g.tensor_tensor(out=cs[1][:], in0=ib, in1=E128i, op=ALU.mult), ["HIB", "EP"], ["cs1"])
        dve(lambda g: g.tensor_tensor(out=cs[0][:], in0=cs[0][:], in1=cs[1][:], op=ALU.add), ["cs0", "cs1"], ["cs0"])
        dve(lambda g, ib=ib: g.tensor_tensor(out=cs[2][:], in0=ib, in1=E128r, op=ALU.mult), ["HIB", "EP"], ["cs2"])
        dve(lambda g, ia=ia: g.tensor_tensor(out=cs[3][:], in0=ia, in1=E128i, op=ALU.mult), ["HIA", "EP"], ["cs3"])
        dve(lambda g: g.tensor_tensor(out=cs[2][:], in0=cs[2][:], in1=cs[3][:], op=ALU.subtract), ["cs2", "cs3"], ["cs2"])
        dve(lambda g, oa=oa, ck=ck: g.tensor_tensor(out=oa, in0=cs[0][:], in1=HEA[:, :, ck], op=ALU.add),
            ["cs0", "HEA"], ["HIA"])
        dve(lambda g, ob=ob, ck=ck: g.tensor_tensor(out=ob, in0=cs[2][:], in1=HEB[:, :, ck], op=ALU.add),
            ["cs2", "HEB"], ["HIB"])
    P.barrier("xchg")
    s2.close()

    yr_ = y.rearrange("(h s j) d -> s h j d", h=2, s=128, j=8)
    def phase3(h):
        s3o = ExitStack()
        ycb = sb("ycb", [128, 4, 1024], BF16, s3o)
        YG = sb("YG", [128, 8, 512], BF16, s3o)
        s3 = ExitStack()
        sw2 = sb("sw2", [128, 2, 16, 128], BF16, s3)
        zT = sb("zT", [128, 4, 8, 129], BF16, s3)
        Hbf = sb("Hbf", [128, 16, 128], BF16, s3)
        cv = sb("cv", [128, 8, 128], F32, s3)
        ycf = sb("ycf", [128, 512], F32, s3)
        sqf = sb("sqf", [128, 512], F32, s3)
        rb = sb("rb", [128, 1024], F32, s3)
        fw = [sb("fw%d" % i, [128, 8, 8, 16], F32, s3) for i in range(2)]
        P.add("sp", lambda g, h=h: g.dma_start(out=X[:].rearrange("p a b -> p (a b)"), in_=xs[h + 2]),
              reads=[("xs", h + 2)], writes=Xk, dma="x")
        P.add("sp", lambda g, h=h: g.dma_start(out=flat3(U8), in_=u8s[h + 2]), reads=[("u8s", h + 2)], writes=["U8"], dma="u8l")
        P.add("sp", lambda g, h=h: g.dma_start(out=flat3(HLA), in_=hlas[h + 2]), reads=[("hlas", h + 2)], writes=["HLA"], dma="hlal")
        load_G(1)
        norm_to_hT(1)
        sC = ring_load(wig(1), None)
        sV = ring_load(wig(2), None)
        sB = ring_load(wig(0), None)
        dve(lambda g: g.memset(zT[:, :, :, 0:1], 0.0), [], ["zT"])
        if h == 0:
            dve(lambda g: g.tensor_copy(out=zT[:, :, 6:8, 0], in_=HIN[:, 64:72].rearrange("p (a b) -> p a b", a=4)),
                ["HIN", "zT"], ["zT"])
        else:
            dve(lambda g: g.tensor_copy(out=zT[:, :, :, 0], in_=zlast[:]), ["zlast", "zT"], ["zT"])
        for cc in range(4):
            for nb in range(2):
                for (slot, bk) in ((sC, 0), (sV, 1)):
                    for kc in range(8):
                        P.add("pe", lambda g, kc=kc, cc=cc, nb=nb, slot=slot, bk=bk: g.matmul(
                            pb[bk][:, :], lhsT=ring[slot][:, kc, cc * 128:(cc + 1) * 128],
                            rhs=hT[:, kc, nb * 512:(nb + 1) * 512], start=(kc == 0), stop=(kc == 7)),
                            reads=[("ring", slot)] + hT_all, writes=[("pb", bk)], silent=(kc < 7))
                act(lambda g: g.copy(out=sqf[:], in_=pb[0][:, :]), [("pb", 0)], ["sqf"])
                dve(lambda g, cc=cc, nb=nb: g.tensor_tensor(
                    out=zT[:, cc, nb * 4:(nb + 1) * 4, 1:129], in0=sqf[:].rearrange("p (a b) -> p a b", a=4),
                    in1=pb[1][:, :].rearrange("p (a b) -> p a b", a=4), op=ALU.mult),
                    ["sqf", ("pb", 1)], ["zT"])
            w0, w1, w2 = (convw_s[:, k * 4 + cc:k * 4 + cc + 1] for k in range(3))
            for (js, a2, a1, a0) in (
                    (slice(2, 8), zT[:, cc, 2:8, 1:129], zT[:, cc, 1:7, 1:129], zT[:, cc, 0:6, 1:129]),
                    (slice(1, 2), zT[:, cc, 1:2, 1:129], zT[:, cc, 0:1, 1:129], zT[:, cc, 7:8, 0:128]),
                    (slice(0, 1), zT[:, cc, 0:1, 1:129], zT[:, cc, 7:8, 0:128], zT[:, cc, 6:7, 0:128])):
                dve(lambda g, js=js, a2=a2, w2=w2: g.tensor_scalar(out=cv[:, js, :], in0=a2, scalar1=w2, scalar2=None,
                                                                   op0=ALU.mult), ["zT", "convw"], ["cv"])
                dve(lambda g, js=js, a1=a1, w1=w1: g.scalar_tensor_tensor(out=cv[:, js, :], in0=a1, scalar=w1, in1=cv[:, js, :],
                                                                          op0=ALU.mult, op1=ALU.add), ["zT", "convw", "cv"], ["cv"])
                dve(lambda g, js=js, a0=a0, w0=w0: g.scalar_tensor_tensor(out=cv[:, js, :], in0=a0, scalar=w0, in1=cv[:, js, :],
                                                                          op0=ALU.mult, op1=ALU.add), ["zT", "convw", "cv"], ["cv"])
            for nb in range(2):
                for kc in range(8):
                    P.add("pe", lambda g, kc=kc, cc=cc, nb=nb: g.matmul(
                        pb[2][:, :], lhsT=ring[sB][:, kc, cc * 128:(cc + 1) * 128],
                        rhs=hT[:, kc, nb * 512:(nb + 1) * 512], start=(kc == 0), stop=(kc == 7)),
                        reads=[("ring", sB)] + hT_all, writes=[("pb", 2)], silent=(kc < 7))
                dve(lambda g, nb=nb: g.tensor_tensor(out=ycf[:], in0=pb[2][:, :],
                                                     in1=cv[:, nb * 4:(nb + 1) * 4, :].rearrange("p a b -> p (a b)"), op=ALU.mult),
                    [("pb", 2), "cv"], ["ycf"])
                act(lambda g: g.activation(out=sqf[:], in_=ycf[:], func=AF.Square), ["ycf"], ["sqf"])
                P.add("pe", lambda g, cc=cc, nb=nb: g.matmul(pb[5 + nb][:, :], lhsT=onesf[:], rhs=sqf[:],
                                                             start=(cc == 0), stop=(cc == 3)),
                      reads=["sqf", "onesf"], writes=[("pb", 5 + nb)])
                act(lambda g, cc=cc, nb=nb: g.copy(out=ycb[:, cc, nb * 512:(nb + 1) * 512], in_=ycf[:]), ["ycf"], ["ycb"])
        if h == 0:
            dve(lambda g: g.tensor_copy(out=zlast[:], in_=zT[:, :, :, 128]), ["zT"], ["zlast"])
        for nb in range(2):
            dve(lambda g, nb=nb: g.tensor_scalar(out=rb[:, nb * 512:(nb + 1) * 512], in0=pb[5 + nb][:, :], scalar1=1.0 / 512,
                                                 scalar2=EPS, op0=ALU.mult, op1=ALU.add), [("pb", 5 + nb)], ["rb"])
        act(lambda g: g.activation(out=rb[:], in_=rb[:], func=AF.Sqrt), ["rb"], ["rb"])
        dve(lambda g: g.reciprocal(out=rb[:], in_=rb[:]), ["rb"], ["rb"])
        for cc in range(4):
            dve(lambda g, cc=cc: g.scalar_tensor_tensor(out=ycb[:, cc, :], in0=ycb[:, cc, :], scalar=onc_s[:, cc:cc + 1],
                                                        in1=rb[:], op0=ALU.mult, op1=ALU.mult), ["ycb", "rb", "onc"], ["ycb"])
        for gh in range(2):
            for si in range(2):
                P.add("sp", lambda g, si=si, gh=gh: g.dma_start(out=sw2[:, si].rearrange("p a b -> p (a b)"),
                                                                in_=ssmw[2 + si, :, gh * 2048:(gh + 1) * 2048]),
                      reads=[("ssmw", 2 + si)], writes=[("sw2", si)], dma="sw2%d" % si)
            for q in range(2):
                g0 = gh * 16 + q * 8
                Er8 = V(EP, g0 * 2 * NL + 4, [[2 * NL, 8], [0, 8], [1, 16]])
                Ei8 = V(EP, g0 * 2 * NL + NL + 4, [[2 * NL, 8], [0, 8], [1, 16]])
                hA = V(HIA, g0 * 33 + 16 + h * 8, [[33, 8], [1, 8], [0, 16]])
                hB = V(HIB, g0 * 33 + 16 + h * 8, [[33, 8], [1, 8], [0, 16]])
                dve(lambda g, Er8=Er8, hA=hA: g.tensor_tensor(out=fw[0][:], in0=Er8, in1=hA, op=ALU.mult), ["EP", "HIA"], ["fw0"])
                dve(lambda g, Ei8=Ei8, hB=hB: g.tensor_tensor(out=fw[1][:], in0=Ei8, in1=hB, op=ALU.mult), ["EP", "HIB"], ["fw1"])
                dve(lambda g: g.tensor_tensor(out=fw[0][:], in0=fw[0][:], in1=fw[1][:], op=ALU.add), ["fw0", "fw1"], ["fw0"])
                dve(lambda g, g0=g0, q=q: g.tensor_tensor(out=Hbf[:, q * 8:(q + 1) * 8, :], in0=HLA[:, g0:g0 + 8, :],
                                                          in1=fw[0][:].rearrange("p a b c -> p a (b c)"), op=ALU.add),
                    ["fw0", "HLA"], ["Hbf"])
            for gq in range(4):
                bk = 3 + (gq % 2)
                for g4 in range(4):
                    gl = gq * 4 + g4
                    P.add("pe", lambda g, gl=gl, g4=g4, bk=bk, gh=gh: g.matmul(
                        pb[bk][:, g4 * 128:(g4 + 1) * 128], lhsT=U8[:, gh * 16 + gl, :], rhs=sw2[:, 0, gl, :],
                        start=True, stop=False), reads=[("sw2", 0), "U8"], writes=[("pb", bk)], silent=True)
                    P.add("pe", lambda g, gl=gl, g4=g4, bk=bk: g.matmul(
                        pb[bk][:, g4 * 128:(g4 + 1) * 128], lhsT=Hbf[:, gl, :], rhs=sw2[:, 1, gl, :],
                        start=False, stop=True), reads=[("sw2", 1), "Hbf"], writes=[("pb", bk)], silent=(g4 < 3))
                act(lambda g, bk=bk: g.activation(out=sqf[:], in_=pb[bk][:, :], func=AF.Square), [("pb", bk)], ["sqf"])
                dve(lambda g: g.tensor_scalar(out=sqf[:], in0=sqf[:], scalar1=0.044715, scalar2=1.0, op0=ALU.mult, op1=ALU.add),
                    ["sqf"], ["sqf"])
                dve(lambda g, bk=bk: g.tensor_tensor(out=sqf[:], in0=sqf[:], in1=pb[bk][:, :], op=ALU.mult),
                    ["sqf", ("pb", bk)], ["sqf"])
                act(lambda g: g.activation(out=sqf[:], in_=sqf[:], func=AF.Sigmoid, scale=1.5957691216057308), ["sqf"], ["sqf"])
                gq_abs = gh * 4 + gq
                dve(lambda g, bk=bk, gq_abs=gq_abs: g.tensor_tensor(
                    out=V(YG, gq_abs * 64, [[16, 4], [512, 8], [1, 16]]), in0=V(sqf, 0, [[128, 4], [16, 8], [1, 16]]),
                    in1=V(pb[bk], 0, [[128, 4], [16, 8], [1, 16]]), op=ALU.mult), ["sqf", ("pb", bk)], ["YG"])
        P.barrier("p3a%d" % h)
        s3.close()
        s3 = ExitStack()
        wouts = sb("wouts", [128, 8, 1024], BF16, s3)
        gluws = sb("gluws", [128, 4, 512], BF16, s3)
        ygT = [sb("ygT%d" % i, [128, 4, 128], BF16, s3) for i in range(2)]
        ysT = [sb("ysT%d" % i, [128, 4, 128], BF16, s3) for i in range(2)]
        yss = sb("yss", [128, 512], F32, s3)
        ysn = sb("ysn", [128, 512], BF16, s3)
        sq2 = sb("sq2", [128, 512], F32, s3)
        wor_ = wout.rearrange("p (k n) -> p k n", k=8)
        cast_load(wouts[:, 0:4, :], wor_[:, 0:4, :], "wouts", 4, 1024)
        cast_load(wouts[:, 4:8, :], wor_[:, 4:8, :], "wouts", 4, 1024)
        cast_load(gluws[:], gluw.rearrange("p (k n) -> p k n", k=4), "gluws", 4, 512)
        for j in range(8):
            gi = j % 2
            for cc in range(4):
                P.add("pe", lambda g, cc=cc, j=j: g.transpose(pt[:, cc * 128:(cc + 1) * 128], YG[:, j, cc * 128:(cc + 1) * 128],
                                                              identb[:]), reads=["YG", "identb"], writes=["pt"], silent=(cc < 3))
            act(lambda g, gi=gi: g.copy(out=ygT[gi][:], in_=pt[:, 0:512].rearrange("p (a b) -> p a b", a=4)),
                ["pt"], [("ygT", gi)])
            for cc in range(4):
                P.add("pe", lambda g, cc=cc, gi=gi: g.matmul(pb[0][:, :], lhsT=ygT[gi][:, cc, :], rhs=gluws[:, cc, :],
                                                             start=(cc == 0), stop=False),
                      reads=[("ygT", gi), "gluws"], writes=[("pb", 0)], silent=True)
            P.add("pe", lambda g: g.matmul(pb[0][:, :], lhsT=onesb[0:1, :], rhs=glub_b[0:1, :], start=False, stop=True),
                  reads=["onesb", "glubb"], writes=[("pb", 0)])
            act(lambda g: g.activation(out=sq2[:], in_=pb[0][:, :], func=AF.Sigmoid), [("pb", 0)], ["sq2"])
            dve(lambda g, j=j: g.tensor_tensor(out=yss[:], in0=YG[:, j, :], in1=sq2[:], op=ALU.mult), ["YG", "sq2"], ["yss"])
            c = rot("ss", 64)
            ssa = sst[:, c:c + 1]
            act(lambda g, ssa=ssa: g.activation(out=junk[:, 0:512], in_=yss[:], func=AF.Square, accum_out=ssa),
                ["yss"], ["junk", ("ss", c)])
            rstd_from(ssa, 512.0, ("ss", c), ("ss", c))
            dve(lambda g, ssa=ssa: g.tensor_scalar(out=ysn[:], in0=yss[:], scalar1=ssa, scalar2=None, op0=ALU.mult),
                ["yss", ("ss", c)], ["ysn"])
            for cc in range(4):
                P.add("pe", lambda g, cc=cc: g.transpose(pt[:, cc * 128:(cc + 1) * 128], ysn[:, cc * 128:(cc + 1) * 128], identb[:]),
                      reads=["ysn", "identb"], writes=["pt"], silent=(cc < 3))
            for cc in range(4):
                act(lambda g, cc=cc, gi=gi: g.activation(out=ysT[gi][:, cc, :], in_=pt[:, cc * 128:(cc + 1) * 128],
                                                         func=AF.Copy, scale=ons_s[:, cc:cc + 1]),
                    ["pt", "ons"], [("ysT", gi)])
            for nh in range(2):
                bk = 1 + nh
                for kc in range(8):
                    lh = ycb[:, kc, j * 128:(j + 1) * 128] if kc < 4 else ysT[gi][:, kc - 4, :]
                    P.add("pe", lambda g, kc=kc, nh=nh, bk=bk, lh=lh: g.matmul(
                        pb[bk][:, :], lhsT=lh, rhs=wouts[:, kc, nh * 512:(nh + 1) * 512], start=(kc == 0), stop=(kc == 7)),
                        reads=["ycb", ("ysT", gi), "wouts"], writes=[("pb", bk)], silent=(kc < 7))
                ti = rot("ta", 2)
                dve(lambda g, ti=ti, bk=bk, nh=nh: g.tensor_tensor(out=tmpA[ti][:], in0=pb[bk][:, :],
                                                                  in1=Gc[:, nh * 512:(nh + 1) * 512], op=ALU.mult),
                    [("pb", bk), "Gc"], [("tmpA", ti)])
                dve(lambda g, ti=ti, j=j, nh=nh: g.tensor_tensor(
                    out=X[:, j, nh * 512:(nh + 1) * 512], in0=X[:, j, nh * 512:(nh + 1) * 512],
                    in1=tmpA[ti][:], op=ALU.add), [("tmpA", ti), ("X", j)], [("X", j)])
        P.barrier("p3b%d" % h)
        s3.close()
        s3o.close()
        s4 = ExitStack()
        actT = sb("actT2", [128, 8, 1024], BF16, s4)
        wds = sb("wds2", [128, 8, 1024], BF16, s4)
        ot = [sb("ot%d" % i, [128, 1024], F32, s4) for i in range(2)]
        FG = sb("FG", [128, 1024], F32, s4)
        P.add("sp", lambda g: g.dma_start(out=FG[:], in_=bass.AP(fin.tensor, fin.offset, [[0, 128], [1, 1024]])),
              writes=["FG"], dma="fg")
        load_G(2)
        norm_to_hT(2)
        ffn(1, actT, wds)
        for j in range(8):
            tile = h * 8 + j
            c = rot("ss", 64)
            ssa = sst[:, c:c + 1]
            oi = j % 2
            act(lambda g, j=j, ssa=ssa: g.activation(out=junk[:], in_=X[:, j, :], func=AF.Square, accum_out=ssa),
                [("X", j)], ["junk", ("ss", c)])
            rstd_from(ssa, 1024.0, ("ss", c), ("ss", c))
            dve(lambda g, j=j, ssa=ssa, oi=oi: g.scalar_tensor_tensor(out=ot[oi][:], in0=X[:, j, :], scalar=ssa,
                                                                      in1=FG[:], op0=ALU.mult, op1=ALU.mult),
                [("X", j), ("ss", c), "FG"], [("ot", oi)])
            P.add("sp", lambda g, oi=oi, j=j, h=h: g.dma_start(out=yr_[:, h, j, :], in_=ot[oi][:]),
                  reads=[("ot", oi)], writes=[("y", tile)], dma="y%d" % oi)
        P.barrier("p3c%d" % h)
        s4.close()

    phase3(0)
    phase3(1)
    P.add("sp", lambda g: g.nop(), reads=[("y", t) for t in range(16)], writes=["done"], silent=True)
    block = es.enter_context(nc.Block())
    P.finalize(block)
    es.close()
    return nc


_NC = None


def _consts():
    identf = np.eye(128, dtype=np.float32)
    ii = np.arange(128) // 16
    maskc = (ii[None, :] >= ii[:, None]).astype(np.float32)
    nl = len(MLIST)
    mtab = np.zeros((128, NK), np.float32)
    ptab = np.zeros((128, NK), np.float32)
    lo = np.arange(128) < 64
    m70 = np.arange(7, -1, -1, dtype=np.float32)
    m18 = np.arange(1, 9, dtype=np.float32)
    mtab[:, 0:8] = m70; mtab[:, 8:16] = m70; mtab[:, 16:24] = m18; mtab[:, 24:32] = m18
    mtab[:, 32:32 + nl] = np.array(MLIST, np.float32); mtab[:, 32 + nl:] = np.array(MLIST, np.float32)
    hp = math.pi / 2
    ptab[:, 0:8] = np.where(lo, hp, 0.0)[:, None]
    ptab[:, 8:16] = np.where(lo, math.pi, hp)[:, None]
    ptab[:, 16:24] = np.where(lo, hp, math.pi)[:, None]
    ptab[:, 24:32] = np.where(lo, math.pi, -hp)[:, None]
    ptab[:, 32:32 + nl] = hp
    ptab[:, 32 + nl:] = 0.0
    return identf, maskc, mtab, ptab


def _lay_kn(w, groups):
    w = np.asarray(w, dtype=np.float32)
    kc = w.shape[0] // 128
    w3 = w.reshape(kc, 128, w.shape[1]).transpose(1, 0, 2)
    return np.ascontiguousarray(np.concatenate([w3[:, :, c0:c0 + n].reshape(128, -1) for (c0, n) in groups], axis=1))


_FG = [(0, 512), (512, 512), (1024, 512), (1536, 512), (2048, 512), (2560, 256)]


def _lay_rows(w):
    w = np.asarray(w, dtype=np.float32)
    fc = w.shape[0] // 128
    return np.ascontiguousarray(w.reshape(fc, 128, w.shape[1]).transpose(1, 0, 2).reshape(128, -1))


def make_in_maps(inp, ncores=8):
    f = lambda a: np.ascontiguousarray(np.asarray(a, dtype=np.float32))
    x = f(inp["x"]); cond = f(inp["cond"])
    col8 = lambda v: f(np.asarray(v).reshape(8, 128).T)
    col4 = lambda v: f(np.asarray(v).reshape(4, 128).T)
    rep = lambda a: f(np.concatenate([a, a], axis=0))
    identf, maskc, mtab, ptab = _consts()
    lamre = rep(np.asarray(inp["lambda_re"])[0].T); lamim = rep(np.asarray(inp["lambda_im"])[0].T)
    logdt = f(np.broadcast_to(np.asarray(inp["log_dt"])[0][None, :], (128, 32)))
    bre = rep(np.asarray(inp["ssm_b_re"])[0].transpose(1, 0, 2).reshape(64, 512))
    bim = rep(np.asarray(inp["ssm_b_im"])[0].transpose(1, 0, 2).reshape(64, 512))
    cre = rep(np.asarray(inp["ssm_c_re"])[0].transpose(2, 0, 1).reshape(64, 512))
    cim = rep(np.asarray(inp["ssm_c_im"])[0].transpose(2, 0, 1).reshape(64, 512))
    d = np.asarray(inp["ssm_d"])[0].reshape(32, 16)
    dcol = f(np.tile(d.T, (8, 1)))
    convw = np.asarray(inp["conv_w"])[0]
    convw_l = f(np.concatenate([convw[k].reshape(4, 128).T for k in range(3)], axis=1))
    shared = dict(
        w_mod=_lay_kn(inp["w_mod"][0], [(b * 512, 512) for b in range(18)]), b_mod=f(inp["b_mod"]), n1=col8(inp["ffn1_norm"][0]), n2=col8(inp["mix_norm"][0]),
        n3=col8(inp["ffn2_norm"][0]), fin=f(np.asarray(inp["final_norm"]).reshape(1, 1024)),
        wg1=_lay_kn(inp["ffn1_w_gate"][0], _FG), wu1=_lay_kn(inp["ffn1_w_up"][0], _FG), wd1=_lay_rows(inp["ffn1_w_down"][0]),
        wg2=_lay_kn(inp["ffn2_w_gate"][0], _FG), wu2=_lay_kn(inp["ffn2_w_up"][0], _FG), wd2=_lay_rows(inp["ffn2_w_down"][0]),
        w_in=_lay_kn(inp["w_in"][0], [(g * 512, 512) for g in range(4)]), convw=convw_l, lamre=lamre, lamim=lamim, logdt=logdt,
        bre=bre, bim=bim, cre=cre, cim=cim, dcol=dcol, gluw=_lay_kn(inp["glu_w"][0], [(0, 512)]), glub=f(inp["glu_b"]),
        onc=col4(inp["out_norm_conv"][0]), ons=col4(inp["out_norm_ssm"][0]), wout=_lay_kn(inp["w_out"][0], [(0, 1024)]),
        identf=identf, maskc=maskc, mtab=mtab, ptab=ptab)
    in_maps = []
    for r in range(ncores):
        b, hs = r // 2, r % 2
        sel = np.zeros((128, 8), np.float32)
        if hs == 1:
            sel[:, 0] = 1.0
        m = dict(shared)
        m["x"] = f(x[b, hs * 2048:(hs + 1) * 2048])
        m["xprev"] = f(x[b, 0:2048]) if hs == 1 else np.zeros((2048, 1024), np.float32)
        m["cact_in"] = col8(cond[b])
        m["sel"] = sel
        in_maps.append(m)
    return in_maps


def kernel(**inp):
    global _NC
    in_maps = make_in_maps(inp)
    x = inp["x"]
    if _NC is None:
        _NC = build()
    res = run_bass_kernel_spmd(_NC, in_maps, core_ids=list(range(8)))
    out = np.zeros((4, 4096, 1024), np.float32)
    for r in range(8):
        b, hs = r // 2, r % 2
        out[b, hs * 2048:(hs + 1) * 2048] = np.asarray(res.results[r]["y"], dtype=np.float32)
    return out
```
